# Optimizing a Trainium2 kernel written in Bass

```python
import jax, jax.numpy as jnp
from jax import lax
import numpy as np

D_MODEL = 1024
BATCH = 8
SEQ = 2048
DEPTH = 2
DEC_BATCH = 128
DEC_SEQ = 1
PAST_LEN = 16384
PAGE_SIZE = 128

EXPAND = 2
D_A = D_MODEL
D_B = D_MODEL
D_AB = D_A + D_B
CONV_A_WIDTH = 31
CONV_B_WIDTH = 3
D_C = EXPAND * D_MODEL
N_C_HEADS = 8
C_HEAD_DIM = D_C // N_C_HEADS
CHUNK = 128
N_EVEN = (DEPTH + 1) // 2
N_ODD = DEPTH // 2
AB_SPLITS = (D_A, 2 * D_A, 2 * D_A + D_B, 2 * D_A + 2 * D_B, 2 * D_A + 3 * D_B)
AB_IN = 2 * D_A + 3 * D_B + D_AB
C_IN = 3 * D_C
DEEPNORM_ALPHA = (2 * DEPTH) ** 0.25
DEEPNORM_BETA = (8 * DEPTH) ** -0.25
LN_EPS = 1e-5

kernel_name = "hybrid_conformer_shortconv_chunkgmlp_decode_step"


def layer_norm(x, g, b):
    xf = x.astype(jnp.float32)
    mu = jnp.mean(xf, axis=-1, keepdims=True)
    var = jnp.mean(jnp.square(xf - mu), axis=-1, keepdims=True)
    y = (xf - mu) * lax.rsqrt(var + LN_EPS) * g.astype(jnp.float32) + b.astype(jnp.float32)
    return y.astype(x.dtype)


def causal_dwconv(x, buf, w, bias):
    width = w.shape[0]
    xp = jnp.concatenate([buf.astype(x.dtype), x], axis=1)
    y = lax.conv_general_dilated(
        xp, w[:, None, :].astype(x.dtype), window_strides=(1,), padding='VALID',
        dimension_numbers=('NWC', 'WIO', 'NWC'), feature_group_count=x.shape[-1])
    if bias is not None:
        y = y + bias.astype(x.dtype)
    return y, xp[:, xp.shape[1] - (width - 1):]


def ab_mixer(x, buf_a, buf_b, w_in, conv_a_w, conv_a_b, norm_a_g, norm_a_b, conv_b_w, w_out):
    z = jnp.einsum('bld,de->ble', x, w_in)
    a_val, a_gate, b_post, c_pre, h, g = jnp.split(z, AB_SPLITS, axis=-1)
    a = a_val * jax.nn.sigmoid(a_gate)
    a, new_a = causal_dwconv(a, buf_a, conv_a_w, conv_a_b)
    a = jax.nn.silu(layer_norm(a, norm_a_g, norm_a_b))
    s, new_b = causal_dwconv(c_pre * h, buf_b, conv_b_w, None)
    bo = b_post * s
    y = jnp.concatenate([a, bo], axis=-1) * jax.nn.silu(g)
    return jnp.einsum('ble,ed->bld', y, w_out), new_a, new_b


def c_mixer(x, w_in, w_s, b_s, norm_v_g, norm_v_b, w_out):
    z = jnp.einsum('bld,de->ble', x, w_in)
    u, v, g = jnp.split(z, (D_C, 2 * D_C), axis=-1)
    u = jax.nn.gelu(u)
    v = layer_norm(jax.nn.gelu(v), norm_v_g, norm_v_b)
    bsz, seq = v.shape[0], v.shape[1]
    n_chunks = -(-seq // CHUNK)
    pad = n_chunks * CHUNK - seq
    vc = jnp.pad(v, ((0, 0), (0, pad), (0, 0))).reshape(bsz, n_chunks, CHUNK, N_C_HEADS, C_HEAD_DIM)
    w_causal = w_s * jnp.tril(jnp.ones((CHUNK, CHUNK), w_s.dtype))
    mix = jnp.einsum('hts,bnshd->bnthd', w_causal, vc) + jnp.transpose(b_s)[None, None, :, :, None]
    mix = mix.reshape(bsz, n_chunks * CHUNK, D_C)[:, :seq]
    y = u * mix * jax.nn.silu(g)
    return jnp.einsum('ble,ed->bld', y, w_out), v


def setup_inputs(seed: int = 0) -> dict:
    key = jax.random.key(seed)
    ks = jax.random.split(key, 20)
    f32 = jnp.float32
    nrm = lambda k, shape, s: jax.random.normal(k, shape, f32) * s
    return {
        "x_prompt": nrm(ks[0], (BATCH, SEQ, D_MODEL), 1.0),
        "x_sample": nrm(ks[1], (DEC_BATCH, DEC_SEQ, D_MODEL), 1.0),
        "cache_conv_a": nrm(ks[2], (N_EVEN, DEC_BATCH, CONV_A_WIDTH - 1, D_A), 0.5),
        "cache_conv_b": nrm(ks[3], (N_EVEN, DEC_BATCH, CONV_B_WIDTH - 1, D_B), 1.0),
        "w_in_ab": nrm(ks[4], (N_EVEN, D_MODEL, AB_IN), D_MODEL ** -0.5),
        "conv_a_w": nrm(ks[5], (N_EVEN, CONV_A_WIDTH, D_A), CONV_A_WIDTH ** -0.5),
        "conv_a_b": nrm(ks[6], (N_EVEN, D_A), 0.02),
        "norm_a_g": 1.0 + nrm(ks[7], (N_EVEN, D_A), 0.05),
        "norm_a_b": nrm(ks[8], (N_EVEN, D_A), 0.02),
        "conv_b_w": nrm(ks[9], (N_EVEN, CONV_B_WIDTH, D_B), CONV_B_WIDTH ** -0.5),
        "w_out_ab": nrm(ks[10], (N_EVEN, D_AB, D_MODEL), DEEPNORM_BETA * D_AB ** -0.5),
        "w_in_c": nrm(ks[11], (N_ODD, D_MODEL, C_IN), D_MODEL ** -0.5),
        "w_s_c": nrm(ks[12], (N_ODD, N_C_HEADS, CHUNK, CHUNK), CHUNK ** -0.5),
        "b_s_c": 1.0 + nrm(ks[13], (N_ODD, N_C_HEADS, CHUNK), 0.1),
        "norm_v_g": 1.0 + nrm(ks[14], (N_ODD, D_C), 0.05),
        "norm_v_b": nrm(ks[15], (N_ODD, D_C), 0.02),
        "w_out_c": nrm(ks[16], (N_ODD, D_C, D_MODEL), DEEPNORM_BETA * D_C ** -0.5),
        "ln_g": 1.0 + nrm(ks[17], (DEPTH, D_MODEL), 0.05),
        "ln_b": nrm(ks[18], (DEPTH, D_MODEL), 0.02),
    }


def reference(x_prompt, x_sample, cache_conv_a, cache_conv_b, w_in_ab, conv_a_w, conv_a_b,
              norm_a_g, norm_a_b, conv_b_w, w_out_ab, w_in_c, w_s_c, b_s_c, norm_v_g, norm_v_b,
              w_out_c, ln_g, ln_b):
    xp, xs = x_prompt, x_sample
    bp = x_prompt.shape[0]
    conv_a_p, conv_a_s, conv_b_p, conv_b_s, v_c_s = [], [], [], [], []
    for l in range(DEPTH):
        if l % 2 == 0:
            e = l // 2
            params = (w_in_ab[e], conv_a_w[e], conv_a_b[e], norm_a_g[e], norm_a_b[e], conv_b_w[e], w_out_ab[e])
            zero_a = jnp.zeros((bp, CONV_A_WIDTH - 1, D_A), xp.dtype)
            zero_b = jnp.zeros((bp, CONV_B_WIDTH - 1, D_B), xp.dtype)
            out_p, na_p, nb_p = ab_mixer(xp, zero_a, zero_b, *params)
            out_s, na_s, nb_s = ab_mixer(xs, cache_conv_a[e], cache_conv_b[e], *params)
            conv_a_p.append(na_p); conv_b_p.append(nb_p)
            conv_a_s.append(na_s); conv_b_s.append(nb_s)
        else:
            o = l // 2
            params = (w_in_c[o], w_s_c[o], b_s_c[o], norm_v_g[o], norm_v_b[o], w_out_c[o])
            out_p, _ = c_mixer(xp, *params)
            out_s, v_s = c_mixer(xs, *params)
            v_c_s.append(v_s)
        xp = layer_norm(DEEPNORM_ALPHA * xp + out_p, ln_g[l], ln_b[l])
        xs = layer_norm(DEEPNORM_ALPHA * xs + out_s, ln_g[l], ln_b[l])
    return (xp, xs, jnp.stack(conv_a_p), jnp.stack(conv_a_s), jnp.stack(conv_b_p),
            jnp.stack(conv_b_s), jnp.stack(v_c_s))
```

```python
from contextlib import ExitStack
import numpy as np
import concourse.bass as bass
import concourse.mybir as mybir
from concourse.bass_utils import run_bass_kernel_spmd

F32 = mybir.dt.float32
BF16 = mybir.dt.bfloat16
I32 = mybir.dt.int32
AF = mybir.ActivationFunctionType
ALU = mybir.AluOpType

NCORES = 8
D = 1024
T = 2048
NQ = 4
TQ = 512
NS = 16
WQ = TQ + NS
ALPHA = 4.0 ** 0.25
EPS = 1e-5
KT = 8

ENGS = ["pe", "act", "dve", "pool", "sp"]


class Prog:
    def __init__(self):
        self.ops = {e: [] for e in ENGS}
        self.cnt = {}
        self.lastw = {}
        self.readers = {}
        self.overlap = {}

    def set_overlap(self, a, bs):
        for b in bs:
            self.overlap.setdefault(a, []).append(b)
            self.overlap.setdefault(b, []).append(a)

    def add(self, eng, fn, r=(), w=(), dma=None):
        deps = {}

        def dep(ev):
            if ev is None:
                return
            if deps.get(ev[0], 0) < ev[1]:
                deps[ev[0]] = ev[1]

        w = list(w) + [k for k in r if k.startswith("pb") and k not in w]
        r = [k for k in r if not k.startswith("pb")]
        for k in r:
            dep(self.lastw.get(k))
        for k in w:
            for o in [k] + self.overlap.get(k, []):
                dep(self.lastw.get(o))
                for ev in self.readers.get(o, {}).items():
                    dep(ev)
        if dma:
            sem = ("ld_" + w[0]) if w else ("st_" + (r[0] if r else "misc"))
        else:
            sem = eng
        inc = 16 if dma else 1
        self.cnt[sem] = self.cnt.get(sem, 0) + inc
        ev = (sem, self.cnt[sem])
        self.ops[eng].append((sorted(deps.items()), fn, sem, inc))
        for k in w:
            for o in [k] + self.overlap.get(k, []):
                self.lastw[o] = ev
                self.readers[o] = {}
        for k in r:
            rd = self.readers.setdefault(k, {})
            if rd.get(ev[0], 0) < ev[1]:
                rd[ev[0]] = ev[1]
        return ev


def build_program(debug=False):
    nc = bass.Bass("TRN2", target_bir_lowering=False)
    P = Prog()

    def din(name, shape):
        return nc.dram_tensor(name, list(shape), F32, kind="ExternalInput").ap()

    def dout(name, shape):
        return nc.dram_tensor(name, list(shape), F32, kind="ExternalOutput").ap()

    x_p = din("x_p", [T, D]); x_s = din("x_s", [NS, D])
    cca = din("cca", [NS, 30, D]); ccb = din("ccb", [NS, 2, D])
    w_in_ab = din("w_in_ab", [D, 7168]); conv_a_w = din("conv_a_w", [31, D]); conv_a_b = din("conv_a_b", [1, D])
    norm_a_g = din("norm_a_g", [1, D]); norm_a_b = din("norm_a_b", [1, D]); conv_b_w = din("conv_b_w", [3, D])
    w_out_ab = din("w_out_ab", [2048, D]); w_in_c = din("w_in_c", [D, 6144]); w_s_c = din("w_s_c", [8, 128, 128])
    b_s_c = din("b_s_c", [1, 1024]); norm_v_g = din("norm_v_g", [2048]); norm_v_b = din("norm_v_b", [2048])
    w_out_c = din("w_out_c", [2048, D]); ln_g = din("ln_g", [2, D]); ln_b = din("ln_b", [2, D])
    identd = din("identd", [128, 128]); trild = din("trild", [128, 128])
    y_p = dout("y_p", [T, D]); y_s = dout("y_s", [NS, D]); cap = dout("cap", [30, D]); cas = dout("cas", [NS, 30, D])
    cbp = dout("cbp", [2, D]); cbs = dout("cbs", [NS, 2, D]); vcs = dout("vcs", [NS, 2048])

    es = ExitStack()
    POOLW = 212480 // 4
    pool = es.enter_context(nc.sbuf_tensor("pool", [128, POOLW], F32))
    psum = es.enter_context(nc.psum_tensor("psum", [128, 4096], F32))
    off = [0]

    def alloc(nbytes):
        o = off[0]
        nb = (nbytes + 63) // 64 * 64
        off[0] += nb
        assert off[0] <= POOLW * 4, ("SBUF overflow", off[0])
        return o

    def view(o, nbytes, dt=F32, pat=None, parts=128, **kw):
        ap = pool[0:parts, o // 4:(o + nbytes) // 4]
        if dt is not F32:
            ap = ap.bitcast(dt)
        if pat:
            ap = ap.rearrange(pat, **kw)
        return ap

    def bank(b, n=1):
        return psum[:, b * 512:(b + n) * 512]

    def bank_bf(b):
        return psum[:, b * 512:(b + 1) * 512].bitcast(BF16)

    RA = alloc(81152)
    o = RA
    o_xbf = o; o += 2 * 2048
    o_xtok = o; o += 2 * 4096
    o_xT = o; o += 8448
    o_a2 = o; o += 8928 + 32
    o_ch = o; o += 8480
    o_diag = o; o += 2 * 7936
    o_stat = o; o += 3 * 2112
    o_tmp = o; o += 6 * 2112
    o_sq = o; o += 2 * 1088
    o_tailA = o; o += 1472
    o_tailB = o; o += 576
    o_convs = o; o += 512
    o_smisc = o; o += 512
    assert o - RA <= 81152, o - RA
    o = RA
    o_wv = o; o += 32768
    o_vf = o; o += 8192
    o_vn = o; o += 2 * 4096
    o_o32 = o; o += 2 * 4096
    o_tmpB = o; o += 2 * 2112
    o_nvg = o; o += 8192
    o_nvb = o; o += 8192
    assert o - RA <= 81152, o - RA
    rng0 = {"xT": (o_xT, 8448), "diag0": (o_diag, 7936), "diag1": (o_diag + 7936, 7936),
            "xbf0": (o_xbf, 2048), "xbf1": (o_xbf + 2048, 2048), "xtok0": (o_xtok, 4096), "xtok1": (o_xtok + 4096, 4096),
            "sq0": (o_sq, 1088), "sq1": (o_sq + 1088, 1088), "tailA": (o_tailA, 1472), "tailB": (o_tailB, 576),
            "convs": (o_convs, 512), "sqs": (o_smisc, 512)}
    for j in range(8):
        rng0["a2_%d" % j] = (o_a2 + 1116 * j, 1116)
        rng0["ch_%d" % j] = (o_ch + 1060 * j, 1060)
    for i in range(3):
        rng0["stat%d" % i] = (o_stat + 2112 * i, 2112)
    for i in range(6):
        rng0["tmp%d" % i] = (o_tmp + 2112 * i, 2112)
    rng1 = {"vf": (o_vf, 8192), "vn0": (o_vn, 4096), "vn1": (o_vn + 4096, 4096),
            "o320": (o_o32, 4096), "o321": (o_o32 + 4096, 4096), "tmpB0": (o_tmpB, 2112), "tmpB1": (o_tmpB + 2112, 2112),
            "nvg": (o_nvg, 8192), "nvb": (o_nvb, 8192)}
    rng0["xin2"] = (o_xtok, 2048)
    rng0["xin3"] = (o_xtok + 2048, 2048)
    P.set_overlap("xtok0", ["xin2", "xin3"])
    for c in range(8):
        rng1["wv%d" % c] = (o_wv + 4096 * c, 4096)
    for k0, (a0, n0) in rng0.items():
        ov = [k1 for k1, (a1, n1) in rng1.items() if a0 < a1 + n1 and a1 < a0 + n0]
        if ov:
            P.set_overlap(k0, ov)

    xT = view(o_xT, 8448, BF16, "p (k t) -> p k t", k=8)
    a2 = view(o_a2, 8928, BF16, "p (k t) -> p k t", k=8)
    ch = view(o_ch, 8480, BF16, "p (k t) -> p k t", k=8)
    diag = [view(o_diag + i * 7936, 7936, BF16, "p (k t) -> p k t", k=31) for i in range(2)]
    stat = [view(o_stat + i * 2112, 2112) for i in range(3)]
    xtok = [view(o_xtok + i * 4096, 4096) for i in range(2)]
    tmp = [view(o_tmp + i * 2112, 2112) for i in range(6)]
    xbf = [view(o_xbf + i * 2048, 2048, BF16) for i in range(2)]
    xin = [(xbf[0], "xbf0"), (xbf[1], "xbf1"), (view(o_xtok, 2048, BF16), "xin2"), (view(o_xtok + 2048, 2048, BF16), "xin3")]
    sq = [view(o_sq + i * 1088, 1056, BF16) for i in range(2)]
    tailA = view(o_tailA, 8 * 46 * 4, F32, "p (j t) -> p j t", j=8)
    tailB = view(o_tailB, 8 * 18 * 4, F32, "p (j t) -> p j t", j=8)
    convs = view(o_convs, 512, F32, "p (j t) -> p j t", j=8)
    smisc = view(o_smisc, 512)
    wvc = view(o_wv, 32768, BF16, "p (c k e) -> p c k e", c=8, k=8)
    WVALL = ["wv%d" % c for c in range(8)]
    vf = view(o_vf, 8192)
    vfs = [(vf, "vf"), (view(o_nvb, 8192), "nvb")]
    vn = [view(o_vn + i * 4096, 4096, BF16) for i in range(2)]
    o32 = [view(o_o32 + i * 4096, 4096) for i in range(2)]
    tmpB = [view(o_tmpB + i * 2112, 2112) for i in range(2)]
    nvg = view(o_nvg, 8192)
    nvb = view(o_nvb, 8192)

    yb = view(alloc(16896), 16896, BF16, "p (e t) -> p e t", e=16)
    x1T = view(alloc(8448), 8448, BF16, "p (k t) -> p k t", k=8)
    o_x1 = alloc(5 * 4096)
    x1 = view(o_x1, 5 * 4096, F32, "p (i d) -> p i d", i=5)
    scc = [view(o_x1 + i * 4096, 4096) for i in range(2)]
    swr = view(o_x1 + 2 * 4096, 4096)
    spr = view(o_x1 + 3 * 4096, 4096)
    tailT = view(o_x1 + 4 * 4096, 4096)
    p37 = view(o_x1, 4096)
    wscf = view(o_x1 + 4096, 4096, F32, "p (h s) -> p h s", h=8)
    wscm = view(o_x1 + 2 * 4096, 2048, BF16, "p (h s) -> p h s", h=8)
    trl = view(o_x1 + 2 * 4096 + 2048, 512)
    bsf = view(o_x1 + 3 * 4096, 4096)
    bsh32 = view(o_x1 + 4 * 4096, 4096)
    NWS = 4
    GW = 256
    wst = [view(alloc(4096), 4096, BF16, "p (k e) -> p k e", k=8) for _ in range(NWS)]
    wres = view(alloc(32768), 32768, BF16, "p (e d) -> p e d", e=16)
    lng = [view(alloc(4096), 4096) for _ in range(2)]
    lnb = [view(alloc(4096), 4096) for _ in range(2)]
    prm = view(alloc(8 * 37 * 4), 8 * 37 * 4, F32, "p (j r) -> p j r", j=8)
    prmh = view(alloc(8 * 31 * 4), 8 * 31 * 4, F32, "p (j r) -> p j r", j=8)
    idf = view(alloc(512), 512)
    idb = view(alloc(256), 256, BF16)
    R = view(alloc(2048), 2048, BF16, "p (h t) -> p h t", h=8)
    Rs = view(alloc(256), 256, BF16, "p (h t) -> p h t", h=8)
    diagB = view(alloc(8 * 3 * 256), 8 * 3 * 256, BF16, "p (j k t) -> p j k t", j=8, k=3)
    ones_b = view(alloc(256), 256, BF16)
    onesrow = ones_b[0:1, :]
    w00 = view(alloc(64), 32)
    bsh = view(o_tmp, 2048, BF16)
    bsl = view(o_tmp + 2112, 2048, BF16)
    bssh = view(o_tmp + 3 * 2112, 256, BF16, "p (h t) -> p h t", h=8)
    bssl = view(o_tmp + 3 * 2112 + 256, 256, BF16, "p (h t) -> p h t", h=8)
    rss = view(o_tmp + 3 * 2112 + 512, 256, BF16, "p (h t) -> p h t", h=8)
    rsh = view(o_tmp + 2 * 2112, 2048, BF16)
    L4 = view(alloc(4096), 4096, BF16)
    R4 = view(alloc(2048), 2048, BF16, "p (h t) -> p h t", h=8)
    R4s = view(alloc(256), 256, BF16, "p (h t) -> p h t", h=8)
    mv = [view(alloc(64), 16 * 4) for _ in range(2)]
    bnst = [view(alloc(64), 6 * 4 * 2) for _ in range(2)]
    bnst = [view(alloc(128), 24 * 4) for _ in range(2)]
    m05 = view(alloc(64), 4)
    halo_a = view(alloc(8 * 30 * 2), 8 * 30 * 2, BF16, "p (j t) -> p j t", j=8)
    halo_c = view(alloc(64), 8 * 2 * 2, BF16, "p (j t) -> p j t", j=8)
    print("SBUF bytes used per partition:", off[0], "of", POOLW * 4)
    indA = view(alloc(64), 16)
    indB = view(alloc(64), 64)

    def yk(e):
        return "y%d" % e
    YALL = ["y%d" % e for e in range(16)]
    A2ALL = ["a2_%d" % j for j in range(8)]
    CHALL = ["ch_%d" % j for j in range(8)]

    def dma(eng, stream, out, in_, r=(), w=(), **kw):
        return P.add(eng, lambda e: e.dma_start(out=out, in_=in_, **kw), r=r, w=w, dma=stream)

    def act(out, in_, func, r, w, bias=0.0, scale=1.0):
        return P.add("act", lambda e: e.activation(out=out, in_=in_, func=func, bias=bias, scale=scale), r=r, w=w)

    def tt(eng, out, in0, in1, op, r, w):
        return P.add(eng, lambda e: e.tensor_tensor(out=out, in0=in0, in1=in1, op=op), r=r, w=w)

    def ts(eng, out, in0, s1, s2, op0, op1, r, w):
        if s2 is None:
            return P.add(eng, lambda e: e.tensor_scalar(out=out, in0=in0, scalar1=s1, scalar2=None, op0=op0), r=r, w=w)
        return P.add(eng, lambda e: e.tensor_scalar(out=out, in0=in0, scalar1=s1, scalar2=s2, op0=op0, op1=op1), r=r, w=w)

    def stt(out, in0, scalar, in1, op0, op1, r, w):
        return P.add("dve", lambda e: e.scalar_tensor_tensor(out=out, in0=in0, scalar=scalar, in1=in1, op0=op0, op1=op1), r=r, w=w)

    def cp(eng, out, in_, r, w):
        if eng == "act":
            return P.add(eng, lambda e: e.activation(out=out, in_=in_, func=AF.Copy), r=r, w=w)
        return P.add(eng, lambda e: e.tensor_copy(out=out, in_=in_), r=r, w=w)

    def memset(eng, ap, val, w):
        return P.add(eng, lambda e: e.memset(ap, val), w=w)

    def mm_group(out, pairs, r, w, first=True, last=True):
        def fn(e):
            n = len(pairs)
            ins = None
            for i, (l, rr) in enumerate(pairs):
                ins = e.matmul(out, lhsT=l, rhs=rr, start=(first and i == 0), stop=(last and i == n - 1))
            return ins
        return P.add("pe", fn, r=r, w=w)

    def tr_group(items, r, w):
        def fn(e):
            ins = None
            for (o_, i_, id_) in items:
                ins = e.transpose(out=o_, in_=i_, identity=id_)
            return ins
        return P.add("pe", fn, r=r, w=w)

    dbg_list = []

    def dbg(name, ap, keys, dt=F32):
        if not debug:
            return
        d = nc.dram_tensor("dbg_" + name, list(ap.shape), dt, kind="ExternalOutput").ap()
        dma("sp", "d_out", d, ap, r=keys)
        dbg_list.append("dbg_" + name)

    wst_i = [0]

    pending = []

    def load_wgroup(wsrc, c0):
        s = wst_i[0] % NWS
        wst_i[0] += 1
        dma("pool", "d_w", wst[s], wsrc[:, c0:c0 + GW].rearrange("(k p) e -> p k e", p=128), w=["wst%d" % s])
        if pending:
            pending.pop(0)()
        return s

    def push_wres(wsrc):
        for c in range(4):
            pending.append(lambda c=c: dma("pool", "d_w", wres[:, 4 * c:4 * c + 4, :],
                                           wsrc[512 * c:512 * (c + 1), :].rearrange("(e p) d -> p e d", p=128), w=["wres"]))

    def load_wv_chunk(c):
        dma("pool", "d_w", wvc[:, c, :, :],
            w_in_c[:, 2048 + 256 * c:2048 + 256 * (c + 1)].rearrange("(k p) e -> p k e", p=128), w=["wv%d" % c])

    def push_wv():
        for c in range(3):
            pending.append(lambda c=c: load_wv_chunk(c))

    def flush_pending():
        while pending:
            pending.pop(0)()

    def segs(q):
        sg = [(0, TQ)]
        if q == NQ - 1:
            sg.append((TQ, NS))
        return sg

    pi_i = [0]

    def inproj(src, srckey, wslot, et, q, consumer, nbank=3):
        for si, (c0, n) in enumerate(segs(q)):
            if si == 0:
                b = pi_i[0] % nbank
                pi_i[0] += 1
                pap = bank(b)[:, 0:n]
                pkey = "pb%d" % b
            else:
                pap = psum[:, 3584 + 256:3584 + 256 + n]
                pkey = "pb7"
            mm_group(pap, [(wst[wslot][:, k, et * 128:(et + 1) * 128], src[:, k, c0:c0 + n]) for k in range(KT)],
                     r=["wst%d" % wslot, srckey], w=[pkey])
            consumer(si, c0, n, pap, pkey)

    tmp_i = [0]

    def newtmp():
        i = tmp_i[0] % 6
        tmp_i[0] += 1
        return i

    def dve_rsqrt(mvt, np_, cx, cy, cc, key):
        x = mvt[0:np_, cx:cx + 1]
        y = mvt[0:np_, cy:cy + 1]
        c = mvt[0:np_, cc:cc + 1]
        P.add("dve", lambda e: e.tensor_single_scalar(out=y.bitcast(I32), in_=x.bitcast(I32), scalar=1, op=ALU.arith_shift_right),
              r=[key], w=[key])
        P.add("dve", lambda e: e.tensor_scalar(out=y.bitcast(I32), in0=y.bitcast(I32), scalar1=-1.0, scalar2=float(0x5f3759df),
                                               op0=ALU.mult, op1=ALU.add), r=[key], w=[key])
        for _ in range(2):
            stt(c, y, x, y, ALU.mult, ALU.mult, r=[key], w=[key])
            ts("dve", c, c, -0.5, 1.5, ALU.mult, ALU.add, r=[key], w=[key])
            tt("dve", y, y, c, ALU.mult, r=[key], w=[key])

    aff_pending = []

    def resid_ln(l, i, np_, xres, xres_keys, po, pokeys, outbuf, outkey, bfout=None, bfkey=None):
        mi = i % 2
        stt(outbuf, xres, ALPHA, po, ALU.mult, ALU.add, r=xres_keys + pokeys, w=[outkey])
        for hh in range(2):
            P.add("dve", lambda e, hh=hh: e.bn_stats(out=bnst[mi][0:np_, hh * 6:(hh + 1) * 6], in_=outbuf[:, hh * 512:(hh + 1) * 512]),
                  r=[outkey], w=["bnst%d" % mi])
        P.add("dve", lambda e: e.bn_aggr(out=mv[mi][0:np_, 0:2], in_=bnst[mi][0:np_, 0:12]), r=["bnst%d" % mi], w=["mv%d" % mi])
        ts("dve", mv[mi][0:np_, 2:3], mv[mi][0:np_, 1:2], EPS, None, ALU.add, None, r=["mv%d" % mi], w=["mv%d" % mi])
        dve_rsqrt(mv[mi], np_, 2, 3, 10, "mv%d" % mi)
        ts("dve", outbuf, outbuf, mv[mi][0:np_, 0:1], mv[mi][0:np_, 3:4], ALU.subtract, ALU.mult, r=[outkey, "mv%d" % mi], w=[outkey])
        tt("dve" if l == 0 else "pool", outbuf, outbuf, lng[l][0:np_, :], ALU.mult, r=[outkey, "lng%d" % l], w=[outkey])
        if bfout is not None:
            tt("dve", bfout, outbuf, lnb[l][0:np_, :], ALU.add, r=[outkey, "lnb%d" % l], w=[bfkey])
        if bfout is not None:
            aff_pending.append(lambda: tt("pool", outbuf, outbuf, lnb[l][0:np_, :], ALU.add, r=[outkey, "lnb%d" % l], w=[outkey]))
        else:
            tt("pool", outbuf, outbuf, lnb[l][0:np_, :], ALU.add, r=[outkey, "lnb%d" % l], w=[outkey])

    def prefetch_x_dma(q):
        ntt_ = 5 if q == NQ - 1 else 4
        if q > 0:
            cp("pool", a2[:, :, 0:30], halo_a, r=["halo_a"], w=A2ALL)
        xall = view(o_xbf, 4 * 2048, BF16, "p (i d) -> p i d", i=4)
        dma("pool", "d_x", xall, x_p[q * TQ:(q + 1) * TQ, :].rearrange("(i p) d -> p i d", p=128),
            w=["xbf0", "xbf1", "xin2", "xin3"])

    def prefetch_x_tr(q):
        ntt_ = 5 if q == NQ - 1 else 4
        for i in range(ntt_):
            buf, key = xin[i % 4]
            np_ = 128 if i < 4 else NS
            if i == 4:
                dma("pool", "d_x", buf[0:np_, :], x_s, w=[key])
            pst = bank_bf(0)[:, 0:8 * np_].rearrange("p (k t) -> p k t", k=8)
            tr_group([(pst[:, k, :], buf[0:np_, k * 128:(k + 1) * 128], idb[0:np_, 0:np_]) for k in range(8)],
                     r=[key, "idb"], w=["pb0"])
            cp("act", xT[:, :, i * 128:i * 128 + np_], pst, r=["pb0"], w=["xT"])

    def prologue_deferred():
        for j in range(8):
            tt("dve", diagB[:, j, :, :], idb.unsqueeze(1).to_broadcast([128, 3, 128]),
               prm[:, j, 34:37].unsqueeze(2).to_broadcast([128, 3, 128]), ALU.mult, r=["idb", "prm"], w=["diagB"])
        for h in range(8):
            tt("dve", wscm[:, h, :], wscf[:, h, :], trl, ALU.mult, r=["x1_1", "x1_2"], w=["x1_2"])
        tr_group([(bank_bf(6)[:, h * 128:(h + 1) * 128], wscm[:, h, :], idb) for h in range(8)], r=["x1_2", "idb"], w=["pb6"])
        cp("dve", R, bank_bf(6)[:, 0:1024].rearrange("p (h t) -> p h t", h=8), r=["pb6"], w=["R"])
        for h in range(8):
            ts("dve", Rs[0:16, h, :], idf[0:16, 0:16], w00[0:16, h:h + 1], None, ALU.mult, None, r=["idf", "w00"], w=["Rs"])
        cp("dve", bsh[0:1, 0:1024], bsf[0:1, 0:1024], r=["x1_3"], w=["tmp0"])
        cp("dve", bsh32[0:1, 0:1024], bsh[0:1, 0:1024], r=["tmp0"], w=["x1_4"])
        tt("dve", bsl[0:1, 0:1024], bsf[0:1, 0:1024], bsh32[0:1, 0:1024], ALU.subtract, r=["x1_3", "x1_4"], w=["tmp1"])
        bsh3 = bsh[0:1, 0:1024].rearrange("p (h t) -> p h t", h=8)
        bsl3 = bsl[0:1, 0:1024].rearrange("p (h t) -> p h t", h=8)
        for h in range(8):
            cp("dve", bssh[0:1, h, :], bsh3[0:1, h, 0:1].to_broadcast([1, 16]), r=["tmp0"], w=["tmp3"])
            cp("dve", bssl[0:1, h, :], bsl3[0:1, h, 0:1].to_broadcast([1, 16]), r=["tmp1"], w=["tmp3"])
        mm_group(psum[0:1, 0:512], [(ones_b[:, 0:1], R[:, 0:4, :])], r=["ones", "R"], w=["pb0"])
        mm_group(psum[0:1, 512:1024], [(ones_b[:, 0:1], R[:, 4:8, :])], r=["ones", "R"], w=["pb1"])
        cp("dve", rsh[0:1, 0:1024], psum[0:1, 0:1024], r=["pb0", "pb1"], w=["tmp2"])
        mm_group(psum[0:1, 1024:1152], [(ones_b[0:16, 0:1], Rs[0:16, :, :])], r=["ones", "Rs"], w=["pb2"])
        cp("dve", rss[0:1, :, :], psum[0:1, 1024:1152].rearrange("p (h t) -> p h t", h=8), r=["pb2"], w=["tmp3"])
        nb32 = pool[0:1, o_x1 // 4:o_x1 // 4 + 2048]
        nbh = pool[0:1, (o_x1 + 8192) // 4:(o_x1 + 12288) // 4].bitcast(BF16)
        nbh32 = pool[0:1, (o_x1 + 12288) // 4:(o_x1 + 20480) // 4]
        nbl = pool[0:1, (o_x1 + 12288) // 4:(o_x1 + 16384) // 4].bitcast(BF16)
        dma("sp", "d_in", nb32, norm_v_b.rearrange("(o n) -> o n", o=1), r=[], w=["x1_0", "x1_1"])
        cp("dve", nbh, nb32, r=["x1_0", "x1_1"], w=["x1_2"])
        cp("dve", nbh32, nbh, r=["x1_2"], w=["x1_3", "x1_4"])
        tt("dve", nb32, nb32, nbh32, ALU.subtract, r=["x1_0", "x1_1", "x1_3", "x1_4"], w=["x1_0", "x1_1"])
        cp("dve", nbl, nb32, r=["x1_0", "x1_1"], w=["x1_3"])
        memset("dve", L4, 0.0, w=["L4"])
        memset("dve", R4, 0.0, w=["R4"])
        memset("dve", R4s, 0.0, w=["R4s"])
        memset("dve", L4[0:2, :], 1.0, w=["L4"])
        dma("sp", "d_in", L4[2:3, :], nbh, r=["x1_2"], w=["L4"])
        dma("sp", "d_in", L4[3:4, :], nbl, r=["x1_3"], w=["L4"])
        dma("sp", "d_in", R4[0:1, :, :], bsh3, r=["tmp0"], w=["R4"])
        dma("sp", "d_in", R4[1:2, :, :], bsl3, r=["tmp1"], w=["R4"])
        dma("sp", "d_in", R4[2:3, :, :], rsh[0:1, 0:1024].rearrange("p (h t) -> p h t", h=8), r=["tmp2"], w=["R4"])
        dma("sp", "d_in", R4[3:4, :, :], rsh[0:1, 0:1024].rearrange("p (h t) -> p h t", h=8), r=["tmp2"], w=["R4"])
        dma("sp", "d_in", R4s[0:1, :, :], bssh[0:1, :, :], r=["tmp3"], w=["R4s"])
        dma("sp", "d_in", R4s[1:2, :, :], bssl[0:1, :, :], r=["tmp3"], w=["R4s"])
        dma("sp", "d_in", R4s[2:3, :, :], rss[0:1, :, :], r=["tmp3"], w=["R4s"])
        dma("sp", "d_in", R4s[3:4, :, :], rss[0:1, :, :], r=["tmp3"], w=["R4s"])
        memset("dve", indA, 0.0, w=["indA"])
        for b in range(4):
            memset("dve", indA[32 * b:32 * b + 32, b:b + 1], 1.0, w=["indA"])
        dma("sp", "d_in", indB[0:16, :], identd[0:16, 0:16], w=["indB"])
        dma("sp", "d_in", indB[16:32, :], identd[0:16, 0:16], w=["indB"])


    next_pre = {}

    def prefetch_l0_weights():
        next_pre[("g", 0)] = load_wgroup(w_in_ab, 1024 + 0 * GW)
        next_pre[("v", 0)] = load_wgroup(w_in_ab, 0 + 0 * GW)
        next_pre[("g", 1)] = load_wgroup(w_in_ab, 1024 + 1 * GW)
        next_pre[("v", 1)] = load_wgroup(w_in_ab, 0 + 1 * GW)

    def quarter(q):
        t0 = q * TQ
        ntt = 5 if q == NQ - 1 else 4
        pre = dict(next_pre)
        next_pre.clear()
        if q > 0:
            cp("pool", ch[:, :, 0:2], halo_c, r=["halo_c"], w=CHALL)
        if q == 0:
            dbg("xT", xT, ["xT"], BF16)
            dbg("prm", prm, ["prm"])
            dbg("R", R, ["R"], BF16)

        NE = GW // 128

        def sec_gate_val(pr):
            sg = pre.pop(("g", pr)) if ("g", pr) in pre else load_wgroup(w_in_ab, 1024 + pr * GW)
            sv = pre.pop(("v", pr)) if ("v", pr) in pre else load_wgroup(w_in_ab, 0 + pr * GW)
            gt = {}
            for e in range(NE):
                j = pr * NE + e
                ti = newtmp()
                gt[j] = ti

                def cons_gate(si, c0, n, pap, pkey, ti=ti):
                    act(tmp[ti][:, c0:c0 + n], pap, AF.Tanh, r=[pkey], w=["tmp%d" % ti], scale=0.5)
                inproj(xT, "xT", sg, e, q, cons_gate)
            for e in range(NE):
                j = pr * NE + e
                ti = gt[j]

                def cons_val(si, c0, n, pap, pkey, ti=ti, j=j):
                    stt(a2[:, j, 30 + c0:30 + c0 + n], tmp[ti][:, c0:c0 + n], 1.0, pap, ALU.add, ALU.mult,
                        r=[pkey, "tmp%d" % ti], w=["a2_%d" % j])
                    if q == NQ - 1:
                        if si == 0:
                            stt(tailA[:, j, 0:30], tmp[ti][:, 482:512], 1.0, pap[:, 482:512], ALU.add, ALU.mult,
                                r=[pkey, "tmp%d" % ti], w=["tailA"])
                        else:
                            stt(tailA[:, j, 30:46], tmp[ti][:, c0:c0 + n], 1.0, pap, ALU.add, ALU.mult,
                                r=[pkey, "tmp%d" % ti], w=["tailA"])
                inproj(xT, "xT", sv, e, q, cons_val)

        conv_i = [0]
        diag_i = [0]

        def build_diag(j):
            s = diag_i[0] % 2
            diag_i[0] += 1
            tt("dve", diag[s], idb.unsqueeze(1).to_broadcast([128, 31, 128]),
               prmh[:, j, :].unsqueeze(2).to_broadcast([128, 31, 128]), ALU.mult, r=["idb", "prmh"], w=["diag%d" % s])
            return s

        def conv_a(j, ds):
            b = 3 + conv_i[0] % 2
            conv_i[0] += 1
            mm_group(bank(b), [(diag[ds][:, k, :], a2[:, j, k:k + TQ]) for k in range(31)],
                     r=["diag%d" % ds, "a2_%d" % j], w=["pb%d" % b])
            act(yb[:, j, 0:TQ], bank(b), AF.Identity, r=["pb%d" % b, "prm"], w=[yk(j)], bias=prm[:, j, 31:32])
            s = j % 2
            act(sq[s][:, 0:TQ], yb[:, j, 0:TQ], AF.Square, r=[yk(j)], w=["sq%d" % s])
            P.add("pe", lambda e, j=j: e.matmul(bank(5), lhsT=ones_b, rhs=yb[:, j, 0:TQ], start=(j == 0), stop=(j == 7)),
                  r=[yk(j), "ones"], w=["pb5"])
            P.add("pe", lambda e, j=j, s=s: e.matmul(bank(6), lhsT=ones_b, rhs=sq[s][:, 0:TQ], start=(j == 0), stop=(j == 7)),
                  r=["sq%d" % s, "ones"], w=["pb6"])

        def sec_cpre_h(pr):
            sc = load_wgroup(w_in_ab, 3072 + pr * GW)
            sh = load_wgroup(w_in_ab, 4096 + pr * GW)
            ctmp = {}
            for e in range(NE):
                j = pr * NE + e
                ti = newtmp()
                ctmp[j] = ti

                def cons_c(si, c0, n, pap, pkey, ti=ti):
                    act(tmp[ti][:, c0:c0 + n], pap, AF.Copy, r=[pkey], w=["tmp%d" % ti])
                inproj(xT, "xT", sc, e, q, cons_c)
            for e in range(NE):
                j = pr * NE + e
                ti = ctmp[j]

                def cons_h(si, c0, n, pap, pkey, ti=ti, j=j):
                    tt("dve", ch[:, j, 2 + c0:2 + c0 + n], tmp[ti][:, c0:c0 + n], pap, ALU.mult,
                       r=[pkey, "tmp%d" % ti], w=["ch_%d" % j])
                    if q == NQ - 1:
                        if si == 0:
                            tt("dve", tailB[:, j, 0:2], tmp[ti][:, 510:512], pap[:, 510:512], ALU.mult,
                               r=[pkey, "tmp%d" % ti], w=["tailB"])
                        else:
                            tt("dve", tailB[:, j, 2:18], tmp[ti][:, c0:c0 + n], pap, ALU.mult,
                               r=[pkey, "tmp%d" % ti], w=["tailB"])
                inproj(xT, "xT", sh, e, q, cons_h)

        def sec_b_rest(pr):
            sp_ = load_wgroup(w_in_ab, 2048 + pr * GW)
            sgb = load_wgroup(w_in_ab, 6144 + pr * GW)
            s_tmp = {}
            for e in range(NE):
                j = pr * NE + e
                ti = newtmp()
                s_tmp[j] = ti
                b = 3 + conv_i[0] % 2
                conv_i[0] += 1
                mm_group(bank(b), [(diagB[:, j, k, :], ch[:, j, k:k + TQ]) for k in range(3)],
                         r=["diagB", "ch_%d" % j], w=["pb%d" % b])
                act(tmp[ti][:, 0:TQ], bank(b), AF.Copy, r=["pb%d" % b], w=["tmp%d" % ti])
                if q == NQ - 1:
                    stt(tmp[ti][:, TQ:WQ], tailB[:, j, 2:18], prm[:, j, 36:37], psum[:, 3584 + 384 + j * 16:3584 + 384 + (j + 1) * 16],
                        ALU.mult, ALU.add, r=["tailB", "prm", "pb7"], w=["tmp%d" % ti])
            for e in range(NE):
                j = pr * NE + e
                ti = s_tmp[j]

                def cons_bp(si, c0, n, pap, pkey, ti=ti):
                    tt("dve", tmp[ti][:, c0:c0 + n], tmp[ti][:, c0:c0 + n], pap, ALU.mult,
                       r=[pkey, "tmp%d" % ti], w=["tmp%d" % ti])
                inproj(xT, "xT", sp_, e, q, cons_bp)
            for e in range(NE):
                j = pr * NE + e
                ti = s_tmp[j]
                t2 = newtmp()

                def cons_gb(si, c0, n, pap, pkey, ti=ti, t2=t2, j=j):
                    act(tmp[t2][:, c0:c0 + n], pap, AF.Silu, r=[pkey], w=["tmp%d" % t2])
                    tt("dve", yb[:, 8 + j, c0:c0 + n], tmp[ti][:, c0:c0 + n], tmp[t2][:, c0:c0 + n], ALU.mult,
                       r=["tmp%d" % ti, "tmp%d" % t2], w=[yk(8 + j)])
                inproj(xT, "xT", sgb, e, q, cons_gb)

        def sec_ga(pr):
            sga = load_wgroup(w_in_ab, 5120 + pr * GW)
            for e in range(NE):
                j = pr * NE + e
                t1 = newtmp()
                t2 = newtmp()

                def cons_ga(si, c0, n, pap, pkey, t1=t1, t2=t2, j=j):
                    act(tmp[t1][:, c0:c0 + n], pap, AF.Silu, r=[pkey], w=["tmp%d" % t1])
                    tt("dve", tmp[t2][:, c0:c0 + n], yb[:, j, c0:c0 + n], stat[0][:, c0:c0 + n], ALU.subtract,
                       r=[yk(j), "stat0"], w=["tmp%d" % t2])
                    tt("dve", tmp[t2][:, c0:c0 + n], tmp[t2][:, c0:c0 + n], stat[1][:, c0:c0 + n], ALU.mult,
                       r=["tmp%d" % t2, "stat1"], w=["tmp%d" % t2])
                    act(tmp[t2][:, c0:c0 + n], tmp[t2][:, c0:c0 + n], AF.Silu, r=["tmp%d" % t2, "prm"], w=["tmp%d" % t2],
                        bias=prm[:, j, 33:34], scale=prm[:, j, 32:33])
                    tt("dve", yb[:, j, c0:c0 + n], tmp[t2][:, c0:c0 + n], tmp[t1][:, c0:c0 + n], ALU.mult,
                       r=["tmp%d" % t1, "tmp%d" % t2], w=[yk(j)])
                inproj(xT, "xT", sga, e, q, cons_ga)

        sprs = [(spr, "x1_3"), (tailT, "x1_4")]

        def samp_load(gi):
            if gi == 0:
                memset("pool", swr, 0.0, w=["x1_2"])
                for bb in range(4):
                    dma("sp", "d_in", swr[32 * bb:32 * bb + 30, 0:1024], conv_a_w[0:30, :], w=["x1_2"])
            s = gi % 2
            sp_, spk = sprs[gi % 2]
            memset("pool", scc[s], 0.0, w=["x1_%d" % s])
            for bb in range(4):
                dma("sp", "d_in", scc[s][32 * bb:32 * bb + 30, 0:1024], cca[4 * gi + bb, :, :], w=["x1_%d" % s])
            tt("dve", sp_, scc[s], swr, ALU.mult, r=["x1_%d" % s, "x1_2"], w=[spk])

        def samp_mm(gi):
            sp_, spk = sprs[gi % 2]
            for j in range(8):
                c = 3584 + 128 + j * 16 + gi * 4
                P.add("pe", lambda e, c=c, j=j: e.matmul(psum[:, c:c + 4], lhsT=sp_[:, j * 128:(j + 1) * 128], rhs=indA,
                                                        start=True, stop=True),
                      r=[spk, "indA"], w=["pb7"])

        def samp_fin_a():
            ts("dve", tailA, tailA, 0.5, None, ALU.mult, None, r=["tailA"], w=["tailA"])
            for j in range(8):
                stt(convs[:, j, :], tailA[:, j, 30:46], prm[:, j, 30:31], psum[:, 3584 + 128 + j * 16:3584 + 128 + (j + 1) * 16],
                    ALU.mult, ALU.add, r=["tailA", "prm", "pb7"], w=["convs"])
                ts("dve", yb[:, j, TQ:WQ], convs[:, j, :], prm[:, j, 31:32], None, ALU.add, None, r=["convs", "prm"], w=[yk(j)])

        def samp_load_b():
            dma("sp", "d_in", scc[0][0:16, 0:1024], ccb[:, 0, :], w=["x1_0"])
            dma("sp", "d_in", scc[0][16:32, 0:1024], ccb[:, 1, :], w=["x1_0"])
            dma("sp", "d_in", scc[1][0:16, 0:1024], conv_b_w[0].partition_broadcast(16), w=["x1_1"])
            dma("sp", "d_in", scc[1][16:32, 0:1024], conv_b_w[1].partition_broadcast(16), w=["x1_1"])
            tt("dve", spr[0:32, :], scc[0][0:32, :], scc[1][0:32, :], ALU.mult, r=["x1_0", "x1_1"], w=["x1_3"])

        def samp_mm_b():
            for j in range(8):
                c = 3584 + 384 + j * 16
                P.add("pe", lambda e, c=c, j=j: e.matmul(psum[:, c:c + 16], lhsT=spr[0:32, j * 128:(j + 1) * 128], rhs=indB[0:32, :],
                                                         start=True, stop=True),
                      r=["x1_3", "indB"], w=["pb7"])
            dma("sp", "d_out", cas[:, 0:29, :], cca[:, 1:30, :])
            dma("sp", "d_out", cbs[:, 0, :], ccb[:, 1, :])

        def tails_out():
            pt = psum[0:46, 0:1024]
            tr_group([(pt[:, j * 128:(j + 1) * 128], tailA[:, j, :], idf) for j in range(8)], r=["tailA", "idf"], w=["pb0", "pb1"])
            cp("dve", tailT[0:46, :], pt, r=["pb0", "pb1"], w=["x1_4"])
            dma("sp", "d_out", cap, tailT[0:30, :], r=["x1_4"])
            dma("sp", "d_out", cas[:, 29, :], tailT[30:46, :], r=["x1_4"])
            pt2 = psum[0:18, 0:1024]
            tr_group([(pt2[:, j * 128:(j + 1) * 128], tailB[:, j, :], idf) for j in range(8)], r=["tailB", "idf"], w=["pb0", "pb1"])
            cp("dve", tailT[0:18, :], pt2, r=["pb0", "pb1"], w=["x1_4"])
            dma("sp", "d_out", cbp, tailT[0:2, :], r=["x1_4"])
            dma("sp", "d_out", cbs[:, 1, :], tailT[2:18, :], r=["x1_4"])

        def ln_a_stats():
            if q == NQ - 1:
                sqs = smisc.bitcast(BF16).rearrange("p (j t) -> p j t", j=8)[:, :, 0:NS]
                for j in range(8):
                    tt("dve", sqs[:, j, :], yb[:, j, TQ:WQ], yb[:, j, TQ:WQ], ALU.mult, r=[yk(j)], w=["sqs"])
                mm_group(psum[:, 3584 + 96:3584 + 112], [(ones_b, yb[:, j, TQ:WQ]) for j in range(8)],
                         r=[yk(j) for j in range(8)] + ["ones"], w=["pb7"])
                mm_group(psum[:, 3584 + 112:3584 + 128], [(ones_b, sqs[:, j, :]) for j in range(8)],
                         r=["sqs", "ones"], w=["pb7"])
            for (c0, n) in segs(q):
                p1 = bank(5)[:, 0:n] if c0 == 0 else psum[:, 3584 + 96:3584 + 112]
                p2 = bank(6)[:, 0:n] if c0 == 0 else psum[:, 3584 + 112:3584 + 128]
                k1 = ["pb5"] if c0 == 0 else ["pb7"]
                k2 = ["pb6"] if c0 == 0 else ["pb7"]
                ts("dve", stat[0][:, c0:c0 + n], p1, 1.0 / D, None, ALU.mult, None, r=k1, w=["stat0"])
                tt("dve", stat[2][:, c0:c0 + n], stat[0][:, c0:c0 + n], stat[0][:, c0:c0 + n], ALU.mult, r=["stat0"], w=["stat2"])
                stt(stat[1][:, c0:c0 + n], p2, 1.0 / D, stat[2][:, c0:c0 + n], ALU.mult, ALU.subtract, r=k2 + ["stat2"], w=["stat1"])
                ts("dve", stat[1][:, c0:c0 + n], stat[1][:, c0:c0 + n], EPS, None, ALU.add, None, r=["stat1"], w=["stat1"])
                act(stat[1][:, c0:c0 + n], stat[1][:, c0:c0 + n], AF.Sqrt, r=["stat1"], w=["stat1"])
                P.add("dve", lambda e, c0=c0, n=n: e.reciprocal(out=stat[1][:, c0:c0 + n], in_=stat[1][:, c0:c0 + n]),
                      r=["stat1"], w=["stat1"])

        push_wres(w_out_ab)
        dsl = {}

        def do_conv(j):
            if j + 1 < 8:
                dsl[j + 1] = build_diag(j + 1)
            conv_a(j, dsl[j])
            if q == NQ - 1:
                if j < 4:
                    samp_mm(j)
                    if j + 1 < 4:
                        samp_load(j + 1)
                elif j == 4:
                    samp_fin_a()
                    samp_load_b()
                elif j == 5:
                    samp_mm_b()
            if j == 7:
                cp("pool", halo_a, a2[:, :, 512:542], r=A2ALL, w=["halo_a"])

        sec_gate_val(0)
        dsl[0] = build_diag(0)
        sec_gate_val(1)
        if q == 0:
            prologue_deferred()
        if q == NQ - 1:
            samp_load(0)
        do_conv(0); sec_cpre_h(0); do_conv(1)
        sec_gate_val(2)
        do_conv(2); sec_cpre_h(1); do_conv(3)
        sec_gate_val(3)
        if q == 0:
            dbg("a2", a2, A2ALL, BF16)
        do_conv(4); sec_cpre_h(2); do_conv(5)
        sec_b_rest(0)
        sec_cpre_h(3)
        cp("pool", halo_c, ch[:, :, 512:514], r=CHALL, w=["halo_c"])
        flush_pending()
        do_conv(6)
        sec_b_rest(1)
        sec_b_rest(2)
        do_conv(7)
        ln_a_stats()
        sec_b_rest(3)
        for pr in range(4):
            sec_ga(pr)
        if q == NQ - 1:
            tails_out()
        if q == 0:
            dbg("ch", ch, CHALL, BF16)
            dbg("y0", yb, YALL, BF16)
            dbg("mean", stat[0], ["stat0"])
            dbg("rstd", stat[1], ["stat1"])

        def l0_out(i):
            np_ = 128 if i < 4 else NS
            c0 = i * 128
            pb = 2 * (i % 2)
            po = psum[0:np_, pb * 512:(pb + 2) * 512]
            pk = ["pb%d" % pb, "pb%d" % (pb + 1)]
            e_early = list(range(8, 16)) + list(range(0, 6))
            e_late = [6, 7]
            for hh in range(2):
                mm_group(po[:, hh * 512:(hh + 1) * 512],
                         [(yb[:, e, c0:c0 + np_], wres[:, e, hh * 512:(hh + 1) * 512]) for e in e_early],
                         r=[yk(e) for e in e_early] + ["wres"], w=[pk[hh]], last=False)
            for hh in range(2):
                mm_group(po[:, hh * 512:(hh + 1) * 512],
                         [(yb[:, e, c0:c0 + np_], wres[:, e, hh * 512:(hh + 1) * 512]) for e in e_late],
                         r=[yk(e) for e in e_late] + ["wres"], w=[pk[hh]], first=False)
            s = i % 2
            src = x_p[t0 + c0:t0 + c0 + 128, :] if i < 4 else x_s
            dma("sp", "d_xt", xtok[s][0:np_, :], src, w=["xtok%d" % s])
            stg, stgk = xin[i % 4]
            resid_ln(0, i, np_, xtok[s][0:np_, :], ["xtok%d" % s], po, pk, x1[0:np_, i, :], "x1_%d" % i,
                     bfout=stg[0:np_, :], bfkey=stgk)

        def l0_tr(i):
            np_ = 128 if i < 4 else NS
            c0 = i * 128
            s = i % 2
            stg, stgk = xin[i % 4]
            pst = bank_bf(4)[:, 0:8 * np_].rearrange("p (k t) -> p k t", k=8)
            tr_group([(pst[:, k, :], stg[0:np_, k * 128:(k + 1) * 128], idb[0:np_, 0:np_]) for k in range(8)],
                     r=[stgk, "idb"], w=["pb4"])
            cp("act", x1T[:, :, c0:c0 + np_], pst, r=["pb4"], w=["x1T"])

        for c in range(3, 8):
            load_wv_chunk(c)
        pre1 = [load_wgroup(w_in_c, 4096 + g * GW) for g in range(NWS)]
        for i in range(4):
            l0_out(i)
        for i in range(4):
            l0_tr(i)
        if ntt == 5:
            l0_out(4)
            l0_tr(4)
        if q == 0:
            dbg("x1", x1, ["x1_%d" % i for i in range(4)])
            dbg("x1T", x1T, ["x1T"], BF16)

        dma("sp", "d_in", nvg, norm_v_g.partition_broadcast(128), w=["nvg"])
        pi_i[0] = 0
        tb_i = [0]
        NG1 = 2048 // GW
        hoist = [None]
        for c in range(3):
            load_wv_chunk(c)
        while aff_pending:
            aff_pending.pop(0)()
        def l1_V(i):
            vfx, vk = vfs[i % 2]
            np_ = 128 if i < 4 else NS
            c0 = i * 128
            for cb in range(4):
                vb = cb % 2
                mm_group(psum[0:np_, vb * 512:(vb + 1) * 512],
                         [(x1T[:, k, c0:c0 + np_], wvc[:, 2 * cb:2 * cb + 2, k, :]) for k in range(KT)],
                         r=["x1T"] + WVALL, w=["pb%d" % vb])
                act(vfx[0:np_, cb * 512:(cb + 1) * 512], psum[0:np_, vb * 512:(vb + 1) * 512], AF.Gelu, r=["pb%d" % vb], w=[vk])

        def l1_C(i):
            l1_C1(i)
            l1_C2(i)

        def l1_C1(i):
            vfx, vk = vfs[i % 2]
            np_ = 128 if i < 4 else NS
            vi = i % 2
            mi = i % 2
            for hh in range(4):
                P.add("dve", lambda e, hh=hh: e.bn_stats(out=bnst[mi][0:np_, hh * 6:(hh + 1) * 6], in_=vfx[0:np_, hh * 512:(hh + 1) * 512]),
                      r=[vk], w=["bnst%d" % mi])
            P.add("dve", lambda e: e.bn_aggr(out=mv[mi][0:np_, 4:6], in_=bnst[mi][0:np_, 0:24]), r=["bnst%d" % mi], w=["mv%d" % mi])
            ts("dve", mv[mi][0:np_, 6:7], mv[mi][0:np_, 5:6], EPS, None, ALU.add, None, r=["mv%d" % mi], w=["mv%d" % mi])
            dve_rsqrt(mv[mi], np_, 6, 7, 11, "mv%d" % mi)
            if i == 0:
                ts("dve", vfx[0:np_, :], vfx[0:np_, :], mv[mi][0:np_, 4:5], mv[mi][0:np_, 7:8], ALU.subtract, ALU.mult,
                   r=[vk, "mv%d" % mi], w=[vk])
            else:
                ts("dve", mv[mi][0:np_, 9:10], mv[mi][0:np_, 4:5], mv[mi][0:np_, 7:8], -1.0, ALU.mult, ALU.mult, r=["mv%d" % mi], w=["mv%d" % mi])
                act(vfx[0:np_, :], vfx[0:np_, :], AF.Identity, r=[vk, "mv%d" % mi], w=[vk], bias=mv[mi][0:np_, 9:10], scale=mv[mi][0:np_, 7:8])

        def l1_C2(i):
            vfx, vk = vfs[i % 2]
            np_ = 128 if i < 4 else NS
            vi = i % 2
            if i < 4:
                tt("dve", vn[vi][0:np_, :], vfx[0:np_, :], nvg[0:np_, :], ALU.mult, r=[vk, "nvg"], w=["vn%d" % vi])
            else:
                dma("sp", "d_in", nvb[0:np_, :], norm_v_b.partition_broadcast(np_), w=["nvb"])
                tt("dve", vfx[0:np_, :], vfx[0:np_, :], nvg[0:np_, :], ALU.mult, r=[vk, "nvg"], w=[vk])
                cp("dve", vn[vi][0:np_, :], vfx[0:np_, :], r=[vk], w=["vn%d" % vi])
                tt("dve", vfx[0:np_, :], vfx[0:np_, :], nvb[0:np_, :], ALU.add, r=[vk, "nvb"], w=[vk])
                dma("sp", "d_out", vcs, vfx[0:np_, :], r=[vk])

        def l1_S(i):
            np_ = 128 if i < 4 else NS
            c0 = i * 128
            vi = i % 2
            for b4 in range(4):
                pbm = 2 + b4

                def fn(e, b4=b4, pbm=pbm):
                    ins = None
                    for dd in range(4):
                        dt_ = b4 * 4 + dd
                        h = dt_ // 2
                        out = psum[:, pbm * 512 + dd * 128:pbm * 512 + dd * 128 + np_]
                        if i < 4:
                            rh, b4r = R[:, h, :], R4[:, h, :]
                        else:
                            rh, b4r = Rs[0:NS, h, :], R4s[:, h, :]
                        e.matmul(out, lhsT=vn[vi][0:np_, dt_ * 128:(dt_ + 1) * 128], rhs=rh, start=True, stop=False)
                        ins = e.matmul(out, lhsT=L4[:, dt_ * 128:(dt_ + 1) * 128], rhs=b4r, start=False, stop=True)
                    return ins
                P.add("pe", fn, r=["vn%d" % vi, "R", "Rs", "R4", "R4s", "L4"], w=["pb%d" % pbm])
                pm = psum[:, pbm * 512:(pbm + 1) * 512].rearrange("p (a t) -> p a t", a=4)[:, :, 0:np_]
                yv = yb[:, b4 * 4:b4 * 4 + 4, c0:c0 + np_]
                tt("dve", yv, yv, pm, ALU.mult, r=[yk(b4 * 4 + d_) for d_ in range(4)] + ["pb%d" % pbm],
                   w=[yk(b4 * 4 + d_) for d_ in range(4)])

        def l1_O(i):
            np_ = 128 if i < 4 else NS
            c0 = i * 128
            po = psum[0:np_, 6 * 512:8 * 512]
            pk = ["pb6", "pb7"]
            for hh in range(2):
                mm_group(po[:, hh * 512:(hh + 1) * 512],
                         [(yb[:, e, c0:c0 + np_], wres[:, e, hh * 512:(hh + 1) * 512]) for e in range(16)],
                         r=YALL + ["wres"], w=[pk[hh]])
            s = i % 2
            resid_ln(1, i, np_, x1[0:np_, i, :], ["x1_%d" % i], po, pk, o32[s][0:np_, :], "o32%d" % s)
            dst = y_p[t0 + c0:t0 + c0 + 128, :] if i < 4 else y_s
            dma("sp", "d_out", dst, o32[s][0:np_, :], r=["o32%d" % s])


        def _hoisted():
            l1_V(0)
            l1_C(0)
        hoist[0] = _hoisted
        for g4 in range(NG1):
            sgg = pre1[g4] if g4 < len(pre1) else load_wgroup(w_in_c, 4096 + g4 * GW)
            for e in range(NE):
                et = g4 * NE + e

                def cons_gg(si, c0, n, pap, pkey, et=et):
                    act(yb[:, et, c0:c0 + n], pap, AF.Silu, r=[pkey], w=[yk(et)])
                inproj(x1T, "x1T", sgg, e, q, cons_gg, nbank=4)
            if g4 == 5:
                hoist[0]()
        for g4 in range(NG1):
            su = load_wgroup(w_in_c, g4 * GW)
            for e in range(NE):
                et = g4 * NE + e
                tb = tb_i[0] % 2
                tb_i[0] += 1

                def cons_u(si, c0, n, pap, pkey, et=et, tb=tb):
                    act(tmpB[tb][:, c0:c0 + n], pap, AF.Gelu, r=[pkey], w=["tmpB%d" % tb])
                    tt("dve", yb[:, et, c0:c0 + n], tmpB[tb][:, c0:c0 + n], yb[:, et, c0:c0 + n], ALU.mult,
                       r=["tmpB%d" % tb, yk(et)], w=[yk(et)])
                inproj(x1T, "x1T", su, e, q, cons_u, nbank=4)
        push_wres(w_out_c)
        flush_pending()
        if q == 0:
            dbg("ugs", yb, YALL, BF16)

        if ntt > 1:
            l1_V(1)
        for i in range(ntt):
            if i + 2 < ntt:
                l1_V(i + 2)
            if i + 1 < ntt:
                l1_C1(i + 1)
            l1_S(i)
            if i + 1 < ntt:
                l1_C2(i + 1)
            if i + 2 == ntt - 1 and q + 1 < NQ:
                prefetch_x_dma(q + 1)
                prefetch_l0_weights()
            if i >= 2:
                l1_O(i - 2)
        for i in range(max(ntt - 2, 0), ntt):
            if i == ntt - 1 and q + 1 < NQ:
                prefetch_x_tr(q + 1)
            l1_O(i)

    prefetch_x_dma(0)
    prefetch_l0_weights()
    dma("sp", "d_in", idf, identd, w=["idf"])
    dma("sp", "d_in", trl, trild, w=["x1_2"])
    dma("sp", "d_in", p37[0:31, :], conv_a_w, w=["x1_0"])
    dma("sp", "d_in", p37[31:32, :], conv_a_b, w=["x1_0"])
    dma("sp", "d_in", p37[32:33, :], norm_a_g, w=["x1_0"])
    dma("sp", "d_in", p37[33:34, :], norm_a_b, w=["x1_0"])
    dma("sp", "d_in", p37[34:37, :], conv_b_w, w=["x1_0"])
    for l in range(2):
        dma("sp", "d_in", lng[l], ln_g[l].partition_broadcast(128), w=["lng%d" % l])
        dma("sp", "d_in", lnb[l], ln_b[l].partition_broadcast(128), w=["lnb%d" % l])
    dma("sp", "d_in", wscf, w_s_c.rearrange("h t s -> t h s"), w=["x1_1"])
    dma("sp", "d_in", w00[0:16, :], w_s_c[:, 0, 0].partition_broadcast(16), w=["w00"], allow_slow_non_contiguous=True)
    dma("sp", "d_in", bsf[0:1, 0:1024], b_s_c, w=["x1_3"])

    cp("dve", idb, idf, r=["idf"], w=["idb"])
    memset("dve", ones_b, 1.0, w=["ones"])
    memset("dve", m05, -0.5, w=["m05"])
    memset("dve", a2[:, :, 0:30], 0.0, w=A2ALL)
    memset("dve", ch[:, :, 0:2], 0.0, w=CHALL)
    for j in range(8):
        tr_group([(psum[:, 3584 + j * 37: 3584 + (j + 1) * 37], p37[0:37, j * 128:(j + 1) * 128], idf[0:37, 0:37])],
                 r=["x1_0", "idf"], w=["pb7"])
    cp("dve", prm, psum[:, 3584:3584 + 8 * 37].rearrange("p (j r) -> p j r", j=8), r=["pb7"], w=["prm"])
    ts("dve", prmh, prm[:, :, 0:31], 0.5, None, ALU.mult, None, r=["prm"], w=["prmh"])
    prefetch_x_tr(0)
    for q in range(NQ):
        quarter(q)

    semnames = sorted(P.cnt.keys())
    sems = {n: es.enter_context(nc.semaphore(n)) for n in semnames}
    block = es.enter_context(nc.Block())
    engmap = {"pe": "tensor", "act": "scalar", "dve": "vector", "pool": "gpsimd", "sp": "sync"}

    def make_body(en):
        def body(e):
            waited = {}
            for deps, fn, sem, inc in P.ops[en]:
                for (sn, val) in deps:
                    if en == "pe" and sn == "pe":
                        continue
                    if waited.get(sn, 0) >= val:
                        continue
                    e.wait_ge(sems[sn], val)
                    waited[sn] = val
                ins = fn(e)
                ins.then_inc(sems[sem], inc)
            if en == "sp":
                for sn in semnames:
                    e.wait_ge(sems[sn], P.cnt[sn])
        return body

    for en in ENGS:
        getattr(block, engmap[en])(make_body(en))
    es.close()
    nc.dbg_list = dbg_list
    return nc


_NC_CACHE = {}


def kernel(x_prompt, x_sample, cache_conv_a, cache_conv_b, w_in_ab, conv_a_w, conv_a_b, norm_a_g, norm_a_b,
           conv_b_w, w_out_ab, w_in_c, w_s_c, b_s_c, norm_v_g, norm_v_b, w_out_c, ln_g, ln_b):
    f = lambda a: np.ascontiguousarray(np.asarray(a, dtype=np.float32))
    x_prompt = f(x_prompt); x_sample = f(x_sample)
    cache_conv_a = f(cache_conv_a); cache_conv_b = f(cache_conv_b)
    shared = {
        "w_in_ab": f(w_in_ab)[0], "conv_a_w": f(conv_a_w)[0], "conv_a_b": f(conv_a_b).reshape(1, D),
        "norm_a_g": f(norm_a_g).reshape(1, D), "norm_a_b": f(norm_a_b).reshape(1, D), "conv_b_w": f(conv_b_w)[0],
        "w_out_ab": f(w_out_ab)[0], "w_in_c": f(w_in_c)[0], "w_s_c": f(w_s_c)[0], "b_s_c": f(b_s_c).reshape(1, 1024),
        "norm_v_g": f(norm_v_g).reshape(2048), "norm_v_b": f(norm_v_b).reshape(2048), "w_out_c": f(w_out_c)[0],
        "ln_g": f(ln_g), "ln_b": f(ln_b),
        "identd": np.eye(128, dtype=np.float32), "trild": np.tril(np.ones((128, 128), dtype=np.float32)),
    }
    in_maps = []
    for c in range(NCORES):
        m = dict(shared)
        m["x_p"] = x_prompt[c]
        m["x_s"] = np.ascontiguousarray(x_sample[NS * c:NS * (c + 1), 0, :])
        m["cca"] = np.ascontiguousarray(cache_conv_a[0, NS * c:NS * (c + 1)])
        m["ccb"] = np.ascontiguousarray(cache_conv_b[0, NS * c:NS * (c + 1)])
        in_maps.append(m)
    if "nc" not in _NC_CACHE:
        _NC_CACHE["nc"] = build_program()
    nc = _NC_CACHE["nc"]
    res = run_bass_kernel_spmd(nc, in_maps, core_ids=list(range(NCORES)))
    rs = res.results
    y_prompt = np.stack([rs[c]["y_p"] for c in range(NCORES)], 0)
    y_sample = np.concatenate([rs[c]["y_s"] for c in range(NCORES)], 0)[:, None, :]
    ca_p = np.stack([rs[c]["cap"] for c in range(NCORES)], 0)[None]
    ca_s = np.concatenate([rs[c]["cas"] for c in range(NCORES)], 0)[None]
    cb_p = np.stack([rs[c]["cbp"] for c in range(NCORES)], 0)[None]
    cb_s = np.concatenate([rs[c]["cbs"] for c in range(NCORES)], 0)[None]
    v_s = np.concatenate([rs[c]["vcs"] for c in range(NCORES)], 0)[None, :, None, :]
    return (y_prompt.astype(np.float32), y_sample.astype(np.float32), ca_p.astype(np.float32), ca_s.astype(np.float32),
            cb_p.astype(np.float32), cb_s.astype(np.float32), v_s.astype(np.float32))
```

```python
from contextlib import ExitStack
import numpy as np
import concourse.bass as bass
import concourse.mybir as mybir
from concourse.bass_utils import run_bass_kernel_spmd

F32 = mybir.dt.float32
BF16 = mybir.dt.bfloat16
I32 = mybir.dt.int32
AF = mybir.ActivationFunctionType
ALU = mybir.AluOpType

NCORES = 8
D = 1024
T = 2048
NQ = 4
TQ = 512
NS = 16
WQ = TQ + NS
ALPHA = 4.0 ** 0.25
EPS = 1e-5
KT = 8

ENGS = ["pe", "act", "dve", "pool", "sp"]


class Prog:
    def __init__(self):
        self.ops = {e: [] for e in ENGS}
        self.cnt = {}
        self.lastw = {}
        self.readers = {}
        self.overlap = {}

    def set_overlap(self, a, bs):
        for b in bs:
            self.overlap.setdefault(a, []).append(b)
            self.overlap.setdefault(b, []).append(a)

    def add(self, eng, fn, r=(), w=(), dma=None):
        deps = {}

        def dep(ev):
            if ev is None:
                return
            if deps.get(ev[0], 0) < ev[1]:
                deps[ev[0]] = ev[1]

        w = list(w) + [k for k in r if k.startswith("pb") and k not in w]
        r = [k for k in r if not k.startswith("pb")]
        for k in r:
            dep(self.lastw.get(k))
        for k in w:
            for o in [k] + self.overlap.get(k, []):
                dep(self.lastw.get(o))
                for ev in self.readers.get(o, {}).items():
                    dep(ev)
        if dma:
            sem = ("ld_" + w[0]) if w else ("st_" + (r[0] if r else "misc"))
        else:
            sem = eng
        inc = 16 if dma else 1
        self.cnt[sem] = self.cnt.get(sem, 0) + inc
        ev = (sem, self.cnt[sem])
        self.ops[eng].append((sorted(deps.items()), fn, sem, inc))
        for k in w:
            for o in [k] + self.overlap.get(k, []):
                self.lastw[o] = ev
                self.readers[o] = {}
        for k in r:
            rd = self.readers.setdefault(k, {})
            if rd.get(ev[0], 0) < ev[1]:
                rd[ev[0]] = ev[1]
        return ev


def build_program(debug=False):
    nc = bass.Bass("TRN2", target_bir_lowering=False)
    P = Prog()

    def din(name, shape):
        return nc.dram_tensor(name, list(shape), F32, kind="ExternalInput").ap()

    def dout(name, shape):
        return nc.dram_tensor(name, list(shape), F32, kind="ExternalOutput").ap()

    x_p = din("x_p", [T, D]); x_s = din("x_s", [NS, D])
    cca = din("cca", [NS, 30, D]); ccb = din("ccb", [NS, 2, D])
    w_in_ab = din("w_in_ab", [D, 7168]); conv_a_w = din("conv_a_w", [31, D]); conv_a_b = din("conv_a_b", [1, D])
    norm_a_g = din("norm_a_g", [1, D]); norm_a_b = din("norm_a_b", [1, D]); conv_b_w = din("conv_b_w", [3, D])
    w_out_ab = din("w_out_ab", [2048, D]); w_in_c = din("w_in_c", [D, 6144]); w_s_c = din("w_s_c", [8, 128, 128])
    b_s_c = din("b_s_c", [1, 1024]); norm_v_g = din("norm_v_g", [2048]); norm_v_b = din("norm_v_b", [2048])
    w_out_c = din("w_out_c", [2048, D]); ln_g = din("ln_g", [2, D]); ln_b = din("ln_b", [2, D])
    identd = din("identd", [128, 128]); trild = din("trild", [128, 128])
    y_p = dout("y_p", [T, D]); y_s = dout("y_s", [NS, D]); cap = dout("cap", [30, D]); cas = dout("cas", [NS, 30, D])
    cbp = dout("cbp", [2, D]); cbs = dout("cbs", [NS, 2, D]); vcs = dout("vcs", [NS, 2048])

    es = ExitStack()
    POOLW = 212480 // 4
    pool = es.enter_context(nc.sbuf_tensor("pool", [128, POOLW], F32))
    psum = es.enter_context(nc.psum_tensor("psum", [128, 4096], F32))
    off = [0]

    def alloc(nbytes):
        o = off[0]
        nb = (nbytes + 63) // 64 * 64
        off[0] += nb
        assert off[0] <= POOLW * 4, ("SBUF overflow", off[0])
        return o

    def view(o, nbytes, dt=F32, pat=None, parts=128, **kw):
        ap = pool[0:parts, o // 4:(o + nbytes) // 4]
        if dt is not F32:
            ap = ap.bitcast(dt)
        if pat:
            ap = ap.rearrange(pat, **kw)
        return ap

    def bank(b, n=1):
        return psum[:, b * 512:(b + n) * 512]

    def bank_bf(b):
        return psum[:, b * 512:(b + 1) * 512].bitcast(BF16)

    RA = alloc(81152)
    o = RA
    o_xbf = o; o += 2 * 2048
    o_xtok = o; o += 2 * 4096
    o_xT = o; o += 8448
    o_a2 = o; o += 8928 + 32
    o_ch = o; o += 8480
    o_diag = o; o += 2 * 7936
    o_stat = o; o += 3 * 2112
    o_tmp = o; o += 6 * 2112
    o_sq = o; o += 2 * 1088
    o_tailA = o; o += 1472
    o_tailB = o; o += 576
    o_convs = o; o += 512
    o_smisc = o; o += 512
    assert o - RA <= 81152, o - RA
    o = RA
    o_wv = o; o += 32768
    o_vf = o; o += 8192
    o_vn = o; o += 2 * 4096
    o_o32 = o; o += 2 * 4096
    o_tmpB = o; o += 2 * 2112
    o_nvg = o; o += 8192
    o_nvb = o; o += 8192
    assert o - RA <= 81152, o - RA
    rng0 = {"xT": (o_xT, 8448), "diag0": (o_diag, 7936), "diag1": (o_diag + 7936, 7936),
            "xbf0": (o_xbf, 2048), "xbf1": (o_xbf + 2048, 2048), "xtok0": (o_xtok, 4096), "xtok1": (o_xtok + 4096, 4096),
            "sq0": (o_sq, 1088), "sq1": (o_sq + 1088, 1088), "tailA": (o_tailA, 1472), "tailB": (o_tailB, 576),
            "convs": (o_convs, 512), "sqs": (o_smisc, 512)}
    for j in range(8):
        rng0["a2_%d" % j] = (o_a2 + 1116 * j, 1116)
        rng0["ch_%d" % j] = (o_ch + 1060 * j, 1060)
    for i in range(3):
        rng0["stat%d" % i] = (o_stat + 2112 * i, 2112)
    for i in range(6):
        rng0["tmp%d" % i] = (o_tmp + 2112 * i, 2112)
    rng1 = {"vf": (o_vf, 8192), "vn0": (o_vn, 4096), "vn1": (o_vn + 4096, 4096),
            "o320": (o_o32, 4096), "o321": (o_o32 + 4096, 4096), "tmpB0": (o_tmpB, 2112), "tmpB1": (o_tmpB + 2112, 2112),
            "nvg": (o_nvg, 8192), "nvb": (o_nvb, 8192)}
    rng0["xin2"] = (o_xtok, 2048)
    rng0["xin3"] = (o_xtok + 2048, 2048)
    P.set_overlap("xtok0", ["xin2", "xin3"])
    for c in range(8):
        rng1["wv%d" % c] = (o_wv + 4096 * c, 4096)
    for k0, (a0, n0) in rng0.items():
        ov = [k1 for k1, (a1, n1) in rng1.items() if a0 < a1 + n1 and a1 < a0 + n0]
        if ov:
            P.set_overlap(k0, ov)

    xT = view(o_xT, 8448, BF16, "p (k t) -> p k t", k=8)
    a2 = view(o_a2, 8928, BF16, "p (k t) -> p k t", k=8)
    ch = view(o_ch, 8480, BF16, "p (k t) -> p k t", k=8)
    diag = [view(o_diag + i * 7936, 7936, BF16, "p (k t) -> p k t", k=31) for i in range(2)]
    stat = [view(o_stat + i * 2112, 2112) for i in range(3)]
    xtok = [view(o_xtok + i * 4096, 4096) for i in range(2)]
    tmp = [view(o_tmp + i * 2112, 2112) for i in range(6)]
    xbf = [view(o_xbf + i * 2048, 2048, BF16) for i in range(2)]
    xin = [(xbf[0], "xbf0"), (xbf[1], "xbf1"), (view(o_xtok, 2048, BF16), "xin2"), (view(o_xtok + 2048, 2048, BF16), "xin3")]
    sq = [view(o_sq + i * 1088, 1056, BF16) for i in range(2)]
    tailA = view(o_tailA, 8 * 46 * 4, F32, "p (j t) -> p j t", j=8)
    tailB = view(o_tailB, 8 * 18 * 4, F32, "p (j t) -> p j t", j=8)
    convs = view(o_convs, 512, F32, "p (j t) -> p j t", j=8)
    smisc = view(o_smisc, 512)
    wvc = view(o_wv, 32768, BF16, "p (c k e) -> p c k e", c=8, k=8)
    WVALL = ["wv%d" % c for c in range(8)]
    vf = view(o_vf, 8192)
    vfs = [(vf, "vf"), (view(o_nvb, 8192), "nvb")]
    vn = [view(o_vn + i * 4096, 4096, BF16) for i in range(2)]
    o32 = [view(o_o32 + i * 4096, 4096) for i in range(2)]
    tmpB = [view(o_tmpB + i * 2112, 2112) for i in range(2)]
    nvg = view(o_nvg, 8192)
    nvb = view(o_nvb, 8192)

    yb = view(alloc(16896), 16896, BF16, "p (e t) -> p e t", e=16)
    x1T = view(alloc(8448), 8448, BF16, "p (k t) -> p k t", k=8)
    o_x1 = alloc(5 * 4096)
    x1 = view(o_x1, 5 * 4096, F32, "p (i d) -> p i d", i=5)
    scc = [view(o_x1 + i * 4096, 4096) for i in range(2)]
    swr = view(o_x1 + 2 * 4096, 4096)
    spr = view(o_x1 + 3 * 4096, 4096)
    tailT = view(o_x1 + 4 * 4096, 4096)
    p37 = view(o_x1, 4096)
    wscf = view(o_x1 + 4096, 4096, F32, "p (h s) -> p h s", h=8)
    wscm = view(o_x1 + 2 * 4096, 2048, BF16, "p (h s) -> p h s", h=8)
    trl = view(o_x1 + 2 * 4096 + 2048, 512)
    bsf = view(o_x1 + 3 * 4096, 4096)
    bsh32 = view(o_x1 + 4 * 4096, 4096)
    NWS = 4
    GW = 256
    wst = [view(alloc(4096), 4096, BF16, "p (k e) -> p k e", k=8) for _ in range(NWS)]
    wres = view(alloc(32768), 32768, BF16, "p (e d) -> p e d", e=16)
    lng = [view(alloc(4096), 4096) for _ in range(2)]
    lnb = [view(alloc(4096), 4096) for _ in range(2)]
    prm = view(alloc(8 * 37 * 4), 8 * 37 * 4, F32, "p (j r) -> p j r", j=8)
    prmh = view(alloc(8 * 31 * 4), 8 * 31 * 4, F32, "p (j r) -> p j r", j=8)
    idf = view(alloc(512), 512)
    idb = view(alloc(256), 256, BF16)
    R = view(alloc(2048), 2048, BF16, "p (h t) -> p h t", h=8)
    Rs = view(alloc(256), 256, BF16, "p (h t) -> p h t", h=8)
    diagB = view(alloc(8 * 3 * 256), 8 * 3 * 256, BF16, "p (j k t) -> p j k t", j=8, k=3)
    ones_b = view(alloc(256), 256, BF16)
    onesrow = ones_b[0:1, :]
    w00 = view(alloc(64), 32)
    bsh = view(o_tmp, 2048, BF16)
    bsl = view(o_tmp + 2112, 2048, BF16)
    bssh = view(o_tmp + 3 * 2112, 256, BF16, "p (h t) -> p h t", h=8)
    bssl = view(o_tmp + 3 * 2112 + 256, 256, BF16, "p (h t) -> p h t", h=8)
    rss = view(o_tmp + 3 * 2112 + 512, 256, BF16, "p (h t) -> p h t", h=8)
    rsh = view(o_tmp + 2 * 2112, 2048, BF16)
    L4 = view(alloc(4096), 4096, BF16)
    R4 = view(alloc(2048), 2048, BF16, "p (h t) -> p h t", h=8)
    R4s = view(alloc(256), 256, BF16, "p (h t) -> p h t", h=8)
    mv = [view(alloc(64), 16 * 4) for _ in range(2)]
    bnst = [view(alloc(64), 6 * 4 * 2) for _ in range(2)]
    bnst = [view(alloc(128), 24 * 4) for _ in range(2)]
    m05 = view(alloc(64), 4)
    halo_a = view(alloc(8 * 30 * 2), 8 * 30 * 2, BF16, "p (j t) -> p j t", j=8)
    halo_c = view(alloc(64), 8 * 2 * 2, BF16, "p (j t) -> p j t", j=8)
    print("SBUF bytes used per partition:", off[0], "of", POOLW * 4)
    indA = view(alloc(64), 16)
    indB = view(alloc(64), 64)

    def yk(e):
        return "y%d" % e
    YALL = ["y%d" % e for e in range(16)]
    A2ALL = ["a2_%d" % j for j in range(8)]
    CHALL = ["ch_%d" % j for j in range(8)]

    def dma(eng, stream, out, in_, r=(), w=(), **kw):
        return P.add(eng, lambda e: e.dma_start(out=out, in_=in_, **kw), r=r, w=w, dma=stream)

    def act(out, in_, func, r, w, bias=0.0, scale=1.0):
        return P.add("act", lambda e: e.activation(out=out, in_=in_, func=func, bias=bias, scale=scale), r=r, w=w)

    def tt(eng, out, in0, in1, op, r, w):
        return P.add(eng, lambda e: e.tensor_tensor(out=out, in0=in0, in1=in1, op=op), r=r, w=w)

    def ts(eng, out, in0, s1, s2, op0, op1, r, w):
        if s2 is None:
            return P.add(eng, lambda e: e.tensor_scalar(out=out, in0=in0, scalar1=s1, scalar2=None, op0=op0), r=r, w=w)
        return P.add(eng, lambda e: e.tensor_scalar(out=out, in0=in0, scalar1=s1, scalar2=s2, op0=op0, op1=op1), r=r, w=w)

    def stt(out, in0, scalar, in1, op0, op1, r, w):
        return P.add("dve", lambda e: e.scalar_tensor_tensor(out=out, in0=in0, scalar=scalar, in1=in1, op0=op0, op1=op1), r=r, w=w)

    def cp(eng, out, in_, r, w):
        if eng == "act":
            return P.add(eng, lambda e: e.activation(out=out, in_=in_, func=AF.Copy), r=r, w=w)
        return P.add(eng, lambda e: e.tensor_copy(out=out, in_=in_), r=r, w=w)

    def memset(eng, ap, val, w):
        return P.add(eng, lambda e: e.memset(ap, val), w=w)

    def mm_group(out, pairs, r, w, first=True, last=True):
        def fn(e):
            n = len(pairs)
            ins = None
            for i, (l, rr) in enumerate(pairs):
                ins = e.matmul(out, lhsT=l, rhs=rr, start=(first and i == 0), stop=(last and i == n - 1))
            return ins
        return P.add("pe", fn, r=r, w=w)

    def tr_group(items, r, w):
        def fn(e):
            ins = None
            for (o_, i_, id_) in items:
                ins = e.transpose(out=o_, in_=i_, identity=id_)
            return ins
        return P.add("pe", fn, r=r, w=w)

    dbg_list = []

    def dbg(name, ap, keys, dt=F32):
        if not debug:
            return
        d = nc.dram_tensor("dbg_" + name, list(ap.shape), dt, kind="ExternalOutput").ap()
        dma("sp", "d_out", d, ap, r=keys)
        dbg_list.append("dbg_" + name)

    wst_i = [0]

    pending = []

    def load_wgroup(wsrc, c0):
        s = wst_i[0] % NWS
        wst_i[0] += 1
        dma("pool", "d_w", wst[s], wsrc[:, c0:c0 + GW].rearrange("(k p) e -> p k e", p=128), w=["wst%d" % s])
        if pending:
            pending.pop(0)()
        return s

    def push_wres(wsrc):
        for c in range(4):
            pending.append(lambda c=c: dma("pool", "d_w", wres[:, 4 * c:4 * c + 4, :],
                                           wsrc[512 * c:512 * (c + 1), :].rearrange("(e p) d -> p e d", p=128), w=["wres"]))

    def load_wv_chunk(c):
        dma("pool", "d_w", wvc[:, c, :, :],
            w_in_c[:, 2048 + 256 * c:2048 + 256 * (c + 1)].rearrange("(k p) e -> p k e", p=128), w=["wv%d" % c])

    def push_wv():
        for c in range(3):
            pending.append(lambda c=c: load_wv_chunk(c))

    def flush_pending():
        while pending:
            pending.pop(0)()

    def segs(q):
        sg = [(0, TQ)]
        if q == NQ - 1:
            sg.append((TQ, NS))
        return sg

    pi_i = [0]

    def inproj(src, srckey, wslot, et, q, consumer, nbank=3):
        for si, (c0, n) in enumerate(segs(q)):
            if si == 0:
                b = pi_i[0] % nbank
                pi_i[0] += 1
                pap = bank(b)[:, 0:n]
                pkey = "pb%d" % b
            else:
                pap = psum[:, 3584 + 256:3584 + 256 + n]
                pkey = "pb7"
            mm_group(pap, [(wst[wslot][:, k, et * 128:(et + 1) * 128], src[:, k, c0:c0 + n]) for k in range(KT)],
                     r=["wst%d" % wslot, srckey], w=[pkey])
            consumer(si, c0, n, pap, pkey)

    tmp_i = [0]

    def newtmp():
        i = tmp_i[0] % 6
        tmp_i[0] += 1
        return i

    def dve_rsqrt(mvt, np_, cx, cy, cc, key):
        x = mvt[0:np_, cx:cx + 1]
        y = mvt[0:np_, cy:cy + 1]
        c = mvt[0:np_, cc:cc + 1]
        P.add("dve", lambda e: e.tensor_single_scalar(out=y.bitcast(I32), in_=x.bitcast(I32), scalar=1, op=ALU.arith_shift_right),
              r=[key], w=[key])
        P.add("dve", lambda e: e.tensor_scalar(out=y.bitcast(I32), in0=y.bitcast(I32), scalar1=-1.0, scalar2=float(0x5f3759df),
                                               op0=ALU.mult, op1=ALU.add), r=[key], w=[key])
        for _ in range(2):
            stt(c, y, x, y, ALU.mult, ALU.mult, r=[key], w=[key])
            ts("dve", c, c, -0.5, 1.5, ALU.mult, ALU.add, r=[key], w=[key])
            tt("dve", y, y, c, ALU.mult, r=[key], w=[key])

    aff_pending = []

    def resid_ln(l, i, np_, xres, xres_keys, po, pokeys, outbuf, outkey, bfout=None, bfkey=None):
        mi = i % 2
        stt(outbuf, xres, ALPHA, po, ALU.mult, ALU.add, r=xres_keys + pokeys, w=[outkey])
        for hh in range(2):
            P.add("dve", lambda e, hh=hh: e.bn_stats(out=bnst[mi][0:np_, hh * 6:(hh + 1) * 6], in_=outbuf[:, hh * 512:(hh + 1) * 512]),
                  r=[outkey], w=["bnst%d" % mi])
        P.add("dve", lambda e: e.bn_aggr(out=mv[mi][0:np_, 0:2], in_=bnst[mi][0:np_, 0:12]), r=["bnst%d" % mi], w=["mv%d" % mi])
        ts("dve", mv[mi][0:np_, 2:3], mv[mi][0:np_, 1:2], EPS, None, ALU.add, None, r=["mv%d" % mi], w=["mv%d" % mi])
        dve_rsqrt(mv[mi], np_, 2, 3, 10, "mv%d" % mi)
        ts("dve", outbuf, outbuf, mv[mi][0:np_, 0:1], mv[mi][0:np_, 3:4], ALU.subtract, ALU.mult, r=[outkey, "mv%d" % mi], w=[outkey])
        tt("dve" if l == 0 else "pool", outbuf, outbuf, lng[l][0:np_, :], ALU.mult, r=[outkey, "lng%d" % l], w=[outkey])
        if bfout is not None:
            tt("dve", bfout, outbuf, lnb[l][0:np_, :], ALU.add, r=[outkey, "lnb%d" % l], w=[bfkey])
        if bfout is not None:
            aff_pending.append(lambda: tt("pool", outbuf, outbuf, lnb[l][0:np_, :], ALU.add, r=[outkey, "lnb%d" % l], w=[outkey]))
        else:
            tt("pool", outbuf, outbuf, lnb[l][0:np_, :], ALU.add, r=[outkey, "lnb%d" % l], w=[outkey])

    def prefetch_x_dma(q):
        ntt_ = 5 if q == NQ - 1 else 4
        if q > 0:
            cp("pool", a2[:, :, 0:30], halo_a, r=["halo_a"], w=A2ALL)
        xall = view(o_xbf, 4 * 2048, BF16, "p (i d) -> p i d", i=4)
        dma("pool", "d_x", xall, x_p[q * TQ:(q + 1) * TQ, :].rearrange("(i p) d -> p i d", p=128),
            w=["xbf0", "xbf1", "xin2", "xin3"])

    def prefetch_x_tr(q):
        ntt_ = 5 if q == NQ - 1 else 4
        for i in range(ntt_):
            buf, key = xin[i % 4]
            np_ = 128 if i < 4 else NS
            if i == 4:
                dma("pool", "d_x", buf[0:np_, :], x_s, w=[key])
            pst = bank_bf(0)[:, 0:8 * np_].rearrange("p (k t) -> p k t", k=8)
            tr_group([(pst[:, k, :], buf[0:np_, k * 128:(k + 1) * 128], idb[0:np_, 0:np_]) for k in range(8)],
                     r=[key, "idb"], w=["pb0"])
            cp("act", xT[:, :, i * 128:i * 128 + np_], pst, r=["pb0"], w=["xT"])

    def prologue_deferred():
        for j in range(8):
            tt("dve", diagB[:, j, :, :], idb.unsqueeze(1).to_broadcast([128, 3, 128]),
               prm[:, j, 34:37].unsqueeze(2).to_broadcast([128, 3, 128]), ALU.mult, r=["idb", "prm"], w=["diagB"])
        for h in range(8):
            tt("dve", wscm[:, h, :], wscf[:, h, :], trl, ALU.mult, r=["x1_1", "x1_2"], w=["x1_2"])
        tr_group([(bank_bf(6)[:, h * 128:(h + 1) * 128], wscm[:, h, :], idb) for h in range(8)], r=["x1_2", "idb"], w=["pb6"])
        cp("dve", R, bank_bf(6)[:, 0:1024].rearrange("p (h t) -> p h t", h=8), r=["pb6"], w=["R"])
        for h in range(8):
            ts("dve", Rs[0:16, h, :], idf[0:16, 0:16], w00[0:16, h:h + 1], None, ALU.mult, None, r=["idf", "w00"], w=["Rs"])
        cp("dve", bsh[0:1, 0:1024], bsf[0:1, 0:1024], r=["x1_3"], w=["tmp0"])
        cp("dve", bsh32[0:1, 0:1024], bsh[0:1, 0:1024], r=["tmp0"], w=["x1_4"])
        tt("dve", bsl[0:1, 0:1024], bsf[0:1, 0:1024], bsh32[0:1, 0:1024], ALU.subtract, r=["x1_3", "x1_4"], w=["tmp1"])
        bsh3 = bsh[0:1, 0:1024].rearrange("p (h t) -> p h t", h=8)
        bsl3 = bsl[0:1, 0:1024].rearrange("p (h t) -> p h t", h=8)
        for h in range(8):
            cp("dve", bssh[0:1, h, :], bsh3[0:1, h, 0:1].to_broadcast([1, 16]), r=["tmp0"], w=["tmp3"])
            cp("dve", bssl[0:1, h, :], bsl3[0:1, h, 0:1].to_broadcast([1, 16]), r=["tmp1"], w=["tmp3"])
        mm_group(psum[0:1, 0:512], [(ones_b[:, 0:1], R[:, 0:4, :])], r=["ones", "R"], w=["pb0"])
        mm_group(psum[0:1, 512:1024], [(ones_b[:, 0:1], R[:, 4:8, :])], r=["ones", "R"], w=["pb1"])
        cp("dve", rsh[0:1, 0:1024], psum[0:1, 0:1024], r=["pb0", "pb1"], w=["tmp2"])
        mm_group(psum[0:1, 1024:1152], [(ones_b[0:16, 0:1], Rs[0:16, :, :])], r=["ones", "Rs"], w=["pb2"])
        cp("dve", rss[0:1, :, :], psum[0:1, 1024:1152].rearrange("p (h t) -> p h t", h=8), r=["pb2"], w=["tmp3"])
        nb32 = pool[0:1, o_x1 // 4:o_x1 // 4 + 2048]
        nbh = pool[0:1, (o_x1 + 8192) // 4:(o_x1 + 12288) // 4].bitcast(BF16)
        nbh32 = pool[0:1, (o_x1 + 12288) // 4:(o_x1 + 20480) // 4]
        nbl = pool[0:1, (o_x1 + 12288) // 4:(o_x1 + 16384) // 4].bitcast(BF16)
        dma("sp", "d_in", nb32, norm_v_b.rearrange("(o n) -> o n", o=1), r=[], w=["x1_0", "x1_1"])
        cp("dve", nbh, nb32, r=["x1_0", "x1_1"], w=["x1_2"])
        cp("dve", nbh32, nbh, r=["x1_2"], w=["x1_3", "x1_4"])
        tt("dve", nb32, nb32, nbh32, ALU.subtract, r=["x1_0", "x1_1", "x1_3", "x1_4"], w=["x1_0", "x1_1"])
        cp("dve", nbl, nb32, r=["x1_0", "x1_1"], w=["x1_3"])
        memset("dve", L4, 0.0, w=["L4"])
        memset("dve", R4, 0.0, w=["R4"])
        memset("dve", R4s, 0.0, w=["R4s"])
        memset("dve", L4[0:2, :], 1.0, w=["L4"])
        dma("sp", "d_in", L4[2:3, :], nbh, r=["x1_2"], w=["L4"])
        dma("sp", "d_in", L4[3:4, :], nbl, r=["x1_3"], w=["L4"])
        dma("sp", "d_in", R4[0:1, :, :], bsh3, r=["tmp0"], w=["R4"])
        dma("sp", "d_in", R4[1:2, :, :], bsl3, r=["tmp1"], w=["R4"])
        dma("sp", "d_in", R4[2:3, :, :], rsh[0:1, 0:1024].rearrange("p (h t) -> p h t", h=8), r=["tmp2"], w=["R4"])
        dma("sp", "d_in", R4[3:4, :, :], rsh[0:1, 0:1024].rearrange("p (h t) -> p h t", h=8), r=["tmp2"], w=["R4"])
        dma("sp", "d_in", R4s[0:1, :, :], bssh[0:1, :, :], r=["tmp3"], w=["R4s"])
        dma("sp", "d_in", R4s[1:2, :, :], bssl[0:1, :, :], r=["tmp3"], w=["R4s"])
        dma("sp", "d_in", R4s[2:3, :, :], rss[0:1, :, :], r=["tmp3"], w=["R4s"])
        dma("sp", "d_in", R4s[3:4, :, :], rss[0:1, :, :], r=["tmp3"], w=["R4s"])
        memset("dve", indA, 0.0, w=["indA"])
        for b in range(4):
            memset("dve", indA[32 * b:32 * b + 32, b:b + 1], 1.0, w=["indA"])
        dma("sp", "d_in", indB[0:16, :], identd[0:16, 0:16], w=["indB"])
        dma("sp", "d_in", indB[16:32, :], identd[0:16, 0:16], w=["indB"])


    next_pre = {}

    def prefetch_l0_weights():
        next_pre[("g", 0)] = load_wgroup(w_in_ab, 1024 + 0 * GW)
        next_pre[("v", 0)] = load_wgroup(w_in_ab, 0 + 0 * GW)
        next_pre[("g", 1)] = load_wgroup(w_in_ab, 1024 + 1 * GW)
        next_pre[("v", 1)] = load_wgroup(w_in_ab, 0 + 1 * GW)

    def quarter(q):
        t0 = q * TQ
        ntt = 5 if q == NQ - 1 else 4
        pre = dict(next_pre)
        next_pre.clear()
        if q > 0:
            cp("pool", ch[:, :, 0:2], halo_c, r=["halo_c"], w=CHALL)
        if q == 0:
            dbg("xT", xT, ["xT"], BF16)
            dbg("prm", prm, ["prm"])
            dbg("R", R, ["R"], BF16)

        NE = GW // 128

        def sec_gate_val(pr):
            sg = pre.pop(("g", pr)) if ("g", pr) in pre else load_wgroup(w_in_ab, 1024 + pr * GW)
            sv = pre.pop(("v", pr)) if ("v", pr) in pre else load_wgroup(w_in_ab, 0 + pr * GW)
            gt = {}
            for e in range(NE):
                j = pr * NE + e
                ti = newtmp()
                gt[j] = ti

                def cons_gate(si, c0, n, pap, pkey, ti=ti):
                    act(tmp[ti][:, c0:c0 + n], pap, AF.Tanh, r=[pkey], w=["tmp%d" % ti], scale=0.5)
                inproj(xT, "xT", sg, e, q, cons_gate)
            for e in range(NE):
                j = pr * NE + e
                ti = gt[j]

                def cons_val(si, c0, n, pap, pkey, ti=ti, j=j):
                    stt(a2[:, j, 30 + c0:30 + c0 + n], tmp[ti][:, c0:c0 + n], 1.0, pap, ALU.add, ALU.mult,
                        r=[pkey, "tmp%d" % ti], w=["a2_%d" % j])
                    if q == NQ - 1:
                        if si == 0:
                            stt(tailA[:, j, 0:30], tmp[ti][:, 482:512], 1.0, pap[:, 482:512], ALU.add, ALU.mult,
                                r=[pkey, "tmp%d" % ti], w=["tailA"])
                        else:
                            stt(tailA[:, j, 30:46], tmp[ti][:, c0:c0 + n], 1.0, pap, ALU.add, ALU.mult,
                                r=[pkey, "tmp%d" % ti], w=["tailA"])
                inproj(xT, "xT", sv, e, q, cons_val)

        conv_i = [0]
        diag_i = [0]

        def build_diag(j):
            s = diag_i[0] % 2
            diag_i[0] += 1
            tt("dve", diag[s], idb.unsqueeze(1).to_broadcast([128, 31, 128]),
               prmh[:, j, :].unsqueeze(2).to_broadcast([128, 31, 128]), ALU.mult, r=["idb", "prmh"], w=["diag%d" % s])
            return s

        def conv_a(j, ds):
            b = 3 + conv_i[0] % 2
            conv_i[0] += 1
            mm_group(bank(b), [(diag[ds][:, k, :], a2[:, j, k:k + TQ]) for k in range(31)],
                     r=["diag%d" % ds, "a2_%d" % j], w=["pb%d" % b])
            act(yb[:, j, 0:TQ], bank(b), AF.Identity, r=["pb%d" % b, "prm"], w=[yk(j)], bias=prm[:, j, 31:32])
            s = j % 2
            act(sq[s][:, 0:TQ], yb[:, j, 0:TQ], AF.Square, r=[yk(j)], w=["sq%d" % s])
            P.add("pe", lambda e, j=j: e.matmul(bank(5), lhsT=ones_b, rhs=yb[:, j, 0:TQ], start=(j == 0), stop=(j == 7)),
                  r=[yk(j), "ones"], w=["pb5"])
            P.add("pe", lambda e, j=j, s=s: e.matmul(bank(6), lhsT=ones_b, rhs=sq[s][:, 0:TQ], start=(j == 0), stop=(j == 7)),
                  r=["sq%d" % s, "ones"], w=["pb6"])

        def sec_cpre_h(pr):
            sc = load_wgroup(w_in_ab, 3072 + pr * GW)
            sh = load_wgroup(w_in_ab, 4096 + pr * GW)
            ctmp = {}
            for e in range(NE):
                j = pr * NE + e
                ti = newtmp()
                ctmp[j] = ti

                def cons_c(si, c0, n, pap, pkey, ti=ti):
                    act(tmp[ti][:, c0:c0 + n], pap, AF.Copy, r=[pkey], w=["tmp%d" % ti])
                inproj(xT, "xT", sc, e, q, cons_c)
            for e in range(NE):
                j = pr * NE + e
                ti = ctmp[j]

                def cons_h(si, c0, n, pap, pkey, ti=ti, j=j):
                    tt("dve", ch[:, j, 2 + c0:2 + c0 + n], tmp[ti][:, c0:c0 + n], pap, ALU.mult,
                       r=[pkey, "tmp%d" % ti], w=["ch_%d" % j])
                    if q == NQ - 1:
                        if si == 0:
                            tt("dve", tailB[:, j, 0:2], tmp[ti][:, 510:512], pap[:, 510:512], ALU.mult,
                               r=[pkey, "tmp%d" % ti], w=["tailB"])
                        else:
                            tt("dve", tailB[:, j, 2:18], tmp[ti][:, c0:c0 + n], pap, ALU.mult,
                               r=[pkey, "tmp%d" % ti], w=["tailB"])
                inproj(xT, "xT", sh, e, q, cons_h)

        def sec_b_rest(pr):
            sp_ = load_wgroup(w_in_ab, 2048 + pr * GW)
            sgb = load_wgroup(w_in_ab, 6144 + pr * GW)
            s_tmp = {}
            for e in range(NE):
                j = pr * NE + e
                ti = newtmp()
                s_tmp[j] = ti
                b = 3 + conv_i[0] % 2
                conv_i[0] += 1
                mm_group(bank(b), [(diagB[:, j, k, :], ch[:, j, k:k + TQ]) for k in range(3)],
                         r=["diagB", "ch_%d" % j], w=["pb%d" % b])
                act(tmp[ti][:, 0:TQ], bank(b), AF.Copy, r=["pb%d" % b], w=["tmp%d" % ti])
                if q == NQ - 1:
                    stt(tmp[ti][:, TQ:WQ], tailB[:, j, 2:18], prm[:, j, 36:37], psum[:, 3584 + 384 + j * 16:3584 + 384 + (j + 1) * 16],
                        ALU.mult, ALU.add, r=["tailB", "prm", "pb7"], w=["tmp%d" % ti])
            for e in range(NE):
                j = pr * NE + e
                ti = s_tmp[j]

                def cons_bp(si, c0, n, pap, pkey, ti=ti):
                    tt("dve", tmp[ti][:, c0:c0 + n], tmp[ti][:, c0:c0 + n], pap, ALU.mult,
                       r=[pkey, "tmp%d" % ti], w=["tmp%d" % ti])
                inproj(xT, "xT", sp_, e, q, cons_bp)
            for e in range(NE):
                j = pr * NE + e
                ti = s_tmp[j]
                t2 = newtmp()

                def cons_gb(si, c0, n, pap, pkey, ti=ti, t2=t2, j=j):
                    act(tmp[t2][:, c0:c0 + n], pap, AF.Silu, r=[pkey], w=["tmp%d" % t2])
                    tt("dve", yb[:, 8 + j, c0:c0 + n], tmp[ti][:, c0:c0 + n], tmp[t2][:, c0:c0 + n], ALU.mult,
                       r=["tmp%d" % ti, "tmp%d" % t2], w=[yk(8 + j)])
                inproj(xT, "xT", sgb, e, q, cons_gb)

        def sec_ga(pr):
            sga = load_wgroup(w_in_ab, 5120 + pr * GW)
            for e in range(NE):
                j = pr * NE + e
                t1 = newtmp()
                t2 = newtmp()

                def cons_ga(si, c0, n, pap, pkey, t1=t1, t2=t2, j=j):
                    act(tmp[t1][:, c0:c0 + n], pap, AF.Silu, r=[pkey], w=["tmp%d" % t1])
                    tt("dve", tmp[t2][:, c0:c0 + n], yb[:, j, c0:c0 + n], stat[0][:, c0:c0 + n], ALU.subtract,
                       r=[yk(j), "stat0"], w=["tmp%d" % t2])
                    tt("dve", tmp[t2][:, c0:c0 + n], tmp[t2][:, c0:c0 + n], stat[1][:, c0:c0 + n], ALU.mult,
                       r=["tmp%d" % t2, "stat1"], w=["tmp%d" % t2])
                    act(tmp[t2][:, c0:c0 + n], tmp[t2][:, c0:c0 + n], AF.Silu, r=["tmp%d" % t2, "prm"], w=["tmp%d" % t2],
                        bias=prm[:, j, 33:34], scale=prm[:, j, 32:33])
                    tt("dve", yb[:, j, c0:c0 + n], tmp[t2][:, c0:c0 + n], tmp[t1][:, c0:c0 + n], ALU.mult,
                       r=["tmp%d" % t1, "tmp%d" % t2], w=[yk(j)])
                inproj(xT, "xT", sga, e, q, cons_ga)

        sprs = [(spr, "x1_3"), (tailT, "x1_4")]

        def samp_load(gi):
            if gi == 0:
                memset("pool", swr, 0.0, w=["x1_2"])
                for bb in range(4):
                    dma("sp", "d_in", swr[32 * bb:32 * bb + 30, 0:1024], conv_a_w[0:30, :], w=["x1_2"])
            s = gi % 2
            sp_, spk = sprs[gi % 2]
            memset("pool", scc[s], 0.0, w=["x1_%d" % s])
            for bb in range(4):
                dma("sp", "d_in", scc[s][32 * bb:32 * bb + 30, 0:1024], cca[4 * gi + bb, :, :], w=["x1_%d" % s])
            tt("dve", sp_, scc[s], swr, ALU.mult, r=["x1_%d" % s, "x1_2"], w=[spk])

        def samp_mm(gi):
            sp_, spk = sprs[gi % 2]
            for j in range(8):
                c = 3584 + 128 + j * 16 + gi * 4
                P.add("pe", lambda e, c=c, j=j: e.matmul(psum[:, c:c + 4], lhsT=sp_[:, j * 128:(j + 1) * 128], rhs=indA,
                                                        start=True, stop=True),
                      r=[spk, "indA"], w=["pb7"])

        def samp_fin_a():
            ts("dve", tailA, tailA, 0.5, None, ALU.mult, None, r=["tailA"], w=["tailA"])
            for j in range(8):
                stt(convs[:, j, :], tailA[:, j, 30:46], prm[:, j, 30:31], psum[:, 3584 + 128 + j * 16:3584 + 128 + (j + 1) * 16],
                    ALU.mult, ALU.add, r=["tailA", "prm", "pb7"], w=["convs"])
                ts("dve", yb[:, j, TQ:WQ], convs[:, j, :], prm[:, j, 31:32], None, ALU.add, None, r=["convs", "prm"], w=[yk(j)])

        def samp_load_b():
            dma("sp", "d_in", scc[0][0:16, 0:1024], ccb[:, 0, :], w=["x1_0"])
            dma("sp", "d_in", scc[0][16:32, 0:1024], ccb[:, 1, :], w=["x1_0"])
            dma("sp", "d_in", scc[1][0:16, 0:1024], conv_b_w[0].partition_broadcast(16), w=["x1_1"])
            dma("sp", "d_in", scc[1][16:32, 0:1024], conv_b_w[1].partition_broadcast(16), w=["x1_1"])
            tt("dve", spr[0:32, :], scc[0][0:32, :], scc[1][0:32, :], ALU.mult, r=["x1_0", "x1_1"], w=["x1_3"])

        def samp_mm_b():
            for j in range(8):
                c = 3584 + 384 + j * 16
                P.add("pe", lambda e, c=c, j=j: e.matmul(psum[:, c:c + 16], lhsT=spr[0:32, j * 128:(j + 1) * 128], rhs=indB[0:32, :],
                                                         start=True, stop=True),
                      r=["x1_3", "indB"], w=["pb7"])
            dma("sp", "d_out", cas[:, 0:29, :], cca[:, 1:30, :])
            dma("sp", "d_out", cbs[:, 0, :], ccb[:, 1, :])

        def tails_out():
            pt = psum[0:46, 0:1024]
            tr_group([(pt[:, j * 128:(j + 1) * 128], tailA[:, j, :], idf) for j in range(8)], r=["tailA", "idf"], w=["pb0", "pb1"])
            cp("dve", tailT[0:46, :], pt, r=["pb0", "pb1"], w=["x1_4"])
            dma("sp", "d_out", cap, tailT[0:30, :], r=["x1_4"])
            dma("sp", "d_out", cas[:, 29, :], tailT[30:46, :], r=["x1_4"])
            pt2 = psum[0:18, 0:1024]
            tr_group([(pt2[:, j * 128:(j + 1) * 128], tailB[:, j, :], idf) for j in range(8)], r=["tailB", "idf"], w=["pb0", "pb1"])
            cp("dve", tailT[0:18, :], pt2, r=["pb0", "pb1"], w=["x1_4"])
            dma("sp", "d_out", cbp, tailT[0:2, :], r=["x1_4"])
            dma("sp", "d_out", cbs[:, 1, :], tailT[2:18, :], r=["x1_4"])

        def ln_a_stats():
            if q == NQ - 1:
                sqs = smisc.bitcast(BF16).rearrange("p (j t) -> p j t", j=8)[:, :, 0:NS]
                for j in range(8):
                    tt("dve", sqs[:, j, :], yb[:, j, TQ:WQ], yb[:, j, TQ:WQ], ALU.mult, r=[yk(j)], w=["sqs"])
                mm_group(psum[:, 3584 + 96:3584 + 112], [(ones_b, yb[:, j, TQ:WQ]) for j in range(8)],
                         r=[yk(j) for j in range(8)] + ["ones"], w=["pb7"])
                mm_group(psum[:, 3584 + 112:3584 + 128], [(ones_b, sqs[:, j, :]) for j in range(8)],
                         r=["sqs", "ones"], w=["pb7"])
            for (c0, n) in segs(q):
                p1 = bank(5)[:, 0:n] if c0 == 0 else psum[:, 3584 + 96:3584 + 112]
                p2 = bank(6)[:, 0:n] if c0 == 0 else psum[:, 3584 + 112:3584 + 128]
                k1 = ["pb5"] if c0 == 0 else ["pb7"]
                k2 = ["pb6"] if c0 == 0 else ["pb7"]
                ts("dve", stat[0][:, c0:c0 + n], p1, 1.0 / D, None, ALU.mult, None, r=k1, w=["stat0"])
                tt("dve", stat[2][:, c0:c0 + n], stat[0][:, c0:c0 + n], stat[0][:, c0:c0 + n], ALU.mult, r=["stat0"], w=["stat2"])
                stt(stat[1][:, c0:c0 + n], p2, 1.0 / D, stat[2][:, c0:c0 + n], ALU.mult, ALU.subtract, r=k2 + ["stat2"], w=["stat1"])
                ts("dve", stat[1][:, c0:c0 + n], stat[1][:, c0:c0 + n], EPS, None, ALU.add, None, r=["stat1"], w=["stat1"])
                act(stat[1][:, c0:c0 + n], stat[1][:, c0:c0 + n], AF.Sqrt, r=["stat1"], w=["stat1"])
                P.add("dve", lambda e, c0=c0, n=n: e.reciprocal(out=stat[1][:, c0:c0 + n], in_=stat[1][:, c0:c0 + n]),
                      r=["stat1"], w=["stat1"])

        push_wres(w_out_ab)
        dsl = {}

        def do_conv(j):
            if j + 1 < 8:
                dsl[j + 1] = build_diag(j + 1)
            conv_a(j, dsl[j])
            if q == NQ - 1:
                if j < 4:
                    samp_mm(j)
                    if j + 1 < 4:
                        samp_load(j + 1)
                elif j == 4:
                    samp_fin_a()
                    samp_load_b()
                elif j == 5:
                    samp_mm_b()
            if j == 7:
                cp("pool", halo_a, a2[:, :, 512:542], r=A2ALL, w=["halo_a"])

        sec_gate_val(0)
        dsl[0] = build_diag(0)
        sec_gate_val(1)
        if q == 0:
            prologue_deferred()
        if q == NQ - 1:
            samp_load(0)
        do_conv(0); sec_cpre_h(0); do_conv(1)
        sec_gate_val(2)
        do_conv(2); sec_cpre_h(1); do_conv(3)
        sec_gate_val(3)
        if q == 0:
            dbg("a2", a2, A2ALL, BF16)
        do_conv(4); sec_cpre_h(2); do_conv(5)
        sec_b_rest(0)
        sec_cpre_h(3)
        cp("pool", halo_c, ch[:, :, 512:514], r=CHALL, w=["halo_c"])
        flush_pending()
        do_conv(6)
        sec_b_rest(1)
        sec_b_rest(2)
        do_conv(7)
        ln_a_stats()
        sec_b_rest(3)
        for pr in range(4):
            sec_ga(pr)
        if q == NQ - 1:
            tails_out()
        if q == 0:
            dbg("ch", ch, CHALL, BF16)
            dbg("y0", yb, YALL, BF16)
            dbg("mean", stat[0], ["stat0"])
            dbg("rstd", stat[1], ["stat1"])

        def l0_out(i):
            np_ = 128 if i < 4 else NS
            c0 = i * 128
            pb = 2 * (i % 2)
            po = psum[0:np_, pb * 512:(pb + 2) * 512]
            pk = ["pb%d" % pb, "pb%d" % (pb + 1)]
            e_early = list(range(8, 16)) + list(range(0, 6))
            e_late = [6, 7]
            for hh in range(2):
                mm_group(po[:, hh * 512:(hh + 1) * 512],
                         [(yb[:, e, c0:c0 + np_], wres[:, e, hh * 512:(hh + 1) * 512]) for e in e_early],
                         r=[yk(e) for e in e_early] + ["wres"], w=[pk[hh]], last=False)
            for hh in range(2):
                mm_group(po[:, hh * 512:(hh + 1) * 512],
                         [(yb[:, e, c0:c0 + np_], wres[:, e, hh * 512:(hh + 1) * 512]) for e in e_late],
                         r=[yk(e) for e in e_late] + ["wres"], w=[pk[hh]], first=False)
            s = i % 2
            src = x_p[t0 + c0:t0 + c0 + 128, :] if i < 4 else x_s
            dma("sp", "d_xt", xtok[s][0:np_, :], src, w=["xtok%d" % s])
            stg, stgk = xin[i % 4]
            resid_ln(0, i, np_, xtok[s][0:np_, :], ["xtok%d" % s], po, pk, x1[0:np_, i, :], "x1_%d" % i,
                     bfout=stg[0:np_, :], bfkey=stgk)

        def l0_tr(i):
            np_ = 128 if i < 4 else NS
            c0 = i * 128
            s = i % 2
            stg, stgk = xin[i % 4]
            pst = bank_bf(4)[:, 0:8 * np_].rearrange("p (k t) -> p k t", k=8)
            tr_group([(pst[:, k, :], stg[0:np_, k * 128:(k + 1) * 128], idb[0:np_, 0:np_]) for k in range(8)],
                     r=[stgk, "idb"], w=["pb4"])
            cp("act", x1T[:, :, c0:c0 + np_], pst, r=["pb4"], w=["x1T"])

        for c in range(3, 8):
            load_wv_chunk(c)
        pre1 = [load_wgroup(w_in_c, 4096 + g * GW) for g in range(NWS)]
        for i in range(4):
            l0_out(i)
        for i in range(4):
            l0_tr(i)
        if ntt == 5:
            l0_out(4)
            l0_tr(4)
        if q == 0:
            dbg("x1", x1, ["x1_%d" % i for i in range(4)])
            dbg("x1T", x1T, ["x1T"], BF16)

        dma("sp", "d_in", nvg, norm_v_g.partition_broadcast(128), w=["nvg"])
        pi_i[0] = 0
        tb_i = [0]
        NG1 = 2048 // GW
        hoist = [None]
        for c in range(3):
            load_wv_chunk(c)
        while aff_pending:
            aff_pending.pop(0)()
        def l1_V(i):
            vfx, vk = vfs[i % 2]
            np_ = 128 if i < 4 else NS
            c0 = i * 128
            for cb in range(4):
                vb = cb % 2
                mm_group(psum[0:np_, vb * 512:(vb + 1) * 512],
                         [(x1T[:, k, c0:c0 + np_], wvc[:, 2 * cb:2 * cb + 2, k, :]) for k in range(KT)],
                         r=["x1T"] + WVALL, w=["pb%d" % vb])
                act(vfx[0:np_, cb * 512:(cb + 1) * 512], psum[0:np_, vb * 512:(vb + 1) * 512], AF.Gelu, r=["pb%d" % vb], w=[vk])

        def l1_C(i):
            l1_C1(i)
            l1_C2(i)

        def l1_C1(i):
            vfx, vk = vfs[i % 2]
            np_ = 128 if i < 4 else NS
            vi = i % 2
            mi = i % 2
            for hh in range(4):
                P.add("dve", lambda e, hh=hh: e.bn_stats(out=bnst[mi][0:np_, hh * 6:(hh + 1) * 6], in_=vfx[0:np_, hh * 512:(hh + 1) * 512]),
                      r=[vk], w=["bnst%d" % mi])
            P.add("dve", lambda e: e.bn_aggr(out=mv[mi][0:np_, 4:6], in_=bnst[mi][0:np_, 0:24]), r=["bnst%d" % mi], w=["mv%d" % mi])
            ts("dve", mv[mi][0:np_, 6:7], mv[mi][0:np_, 5:6], EPS, None, ALU.add, None, r=["mv%d" % mi], w=["mv%d" % mi])
            dve_rsqrt(mv[mi], np_, 6, 7, 11, "mv%d" % mi)
            if i == 0:
                ts("dve", vfx[0:np_, :], vfx[0:np_, :], mv[mi][0:np_, 4:5], mv[mi][0:np_, 7:8], ALU.subtract, ALU.mult,
                   r=[vk, "mv%d" % mi], w=[vk])
            else:
                ts("dve", mv[mi][0:np_, 9:10], mv[mi][0:np_, 4:5], mv[mi][0:np_, 7:8], -1.0, ALU.mult, ALU.mult, r=["mv%d" % mi], w=["mv%d" % mi])
                act(vfx[0:np_, :], vfx[0:np_, :], AF.Identity, r=[vk, "mv%d" % mi], w=[vk], bias=mv[mi][0:np_, 9:10], scale=mv[mi][0:np_, 7:8])

        def l1_C2(i):
            vfx, vk = vfs[i % 2]
            np_ = 128 if i < 4 else NS
            vi = i % 2
            if i < 4:
                tt("dve", vn[vi][0:np_, :], vfx[0:np_, :], nvg[0:np_, :], ALU.mult, r=[vk, "nvg"], w=["vn%d" % vi])
            else:
                dma("sp", "d_in", nvb[0:np_, :], norm_v_b.partition_broadcast(np_), w=["nvb"])
                tt("dve", vfx[0:np_, :], vfx[0:np_, :], nvg[0:np_, :], ALU.mult, r=[vk, "nvg"], w=[vk])
                cp("dve", vn[vi][0:np_, :], vfx[0:np_, :], r=[vk], w=["vn%d" % vi])
                tt("dve", vfx[0:np_, :], vfx[0:np_, :], nvb[0:np_, :], ALU.add, r=[vk, "nvb"], w=[vk])
                dma("sp", "d_out", vcs, vfx[0:np_, :], r=[vk])

        def l1_S(i):
            np_ = 128 if i < 4 else NS
            c0 = i * 128
            vi = i % 2
            for b4 in range(4):
                pbm = 2 + b4

                def fn(e, b4=b4, pbm=pbm):
                    ins = None
                    for dd in range(4):
                        dt_ = b4 * 4 + dd
                        h = dt_ // 2
                        out = psum[:, pbm * 512 + dd * 128:pbm * 512 + dd * 128 + np_]
                        if i < 4:
                            rh, b4r = R[:, h, :], R4[:, h, :]
                        else:
                            rh, b4r = Rs[0:NS, h, :], R4s[:, h, :]
                        e.matmul(out, lhsT=vn[vi][0:np_, dt_ * 128:(dt_ + 1) * 128], rhs=rh, start=True, stop=False)
                        ins = e.matmul(out, lhsT=L4[:, dt_ * 128:(dt_ + 1) * 128], rhs=b4r, start=False, stop=True)
                    return ins
                P.add("pe", fn, r=["vn%d" % vi, "R", "Rs", "R4", "R4s", "L4"], w=["pb%d" % pbm])
            pm = psum[:, 2 * 512:6 * 512].rearrange("p (a t) -> p a t", a=16)[:, :, 0:np_]
            yv = yb[:, 0:16, c0:c0 + np_]
            tt("dve", yv, yv, pm, ALU.mult, r=YALL + ["pb2", "pb3", "pb4", "pb5"], w=YALL)

        def l1_O(i):
            np_ = 128 if i < 4 else NS
            c0 = i * 128
            po = psum[0:np_, 6 * 512:8 * 512]
            pk = ["pb6", "pb7"]
            for hh in range(2):
                mm_group(po[:, hh * 512:(hh + 1) * 512],
                         [(yb[:, e, c0:c0 + np_], wres[:, e, hh * 512:(hh + 1) * 512]) for e in range(16)],
                         r=YALL + ["wres"], w=[pk[hh]])
            s = i % 2
            resid_ln(1, i, np_, x1[0:np_, i, :], ["x1_%d" % i], po, pk, o32[s][0:np_, :], "o32%d" % s)
            dst = y_p[t0 + c0:t0 + c0 + 128, :] if i < 4 else y_s
            dma("sp", "d_out", dst, o32[s][0:np_, :], r=["o32%d" % s])


        def _hoisted():
            l1_V(0)
            l1_C(0)
        hoist[0] = _hoisted
        for g4 in range(NG1):
            sgg = pre1[g4] if g4 < len(pre1) else load_wgroup(w_in_c, 4096 + g4 * GW)
            for e in range(NE):
                et = g4 * NE + e

                def cons_gg(si, c0, n, pap, pkey, et=et):
                    act(yb[:, et, c0:c0 + n], pap, AF.Silu, r=[pkey], w=[yk(et)])
                inproj(x1T, "x1T", sgg, e, q, cons_gg, nbank=4)
            if g4 == 3:
                hoist[0]()
        for g4 in range(NG1):
            su = load_wgroup(w_in_c, g4 * GW)
            for e in range(NE):
                et = g4 * NE + e
                tb = tb_i[0] % 2
                tb_i[0] += 1

                def cons_u(si, c0, n, pap, pkey, et=et, tb=tb):
                    act(tmpB[tb][:, c0:c0 + n], pap, AF.Gelu, r=[pkey], w=["tmpB%d" % tb])
                    tt("dve", yb[:, et, c0:c0 + n], tmpB[tb][:, c0:c0 + n], yb[:, et, c0:c0 + n], ALU.mult,
                       r=["tmpB%d" % tb, yk(et)], w=[yk(et)])
                inproj(x1T, "x1T", su, e, q, cons_u, nbank=4)
        push_wres(w_out_c)
        flush_pending()
        if q == 0:
            dbg("ugs", yb, YALL, BF16)

        if ntt > 1:
            l1_V(1)
        for i in range(ntt):
            if i + 2 < ntt:
                l1_V(i + 2)
            if i + 1 < ntt:
                l1_C1(i + 1)
            l1_S(i)
            if i + 1 < ntt:
                l1_C2(i + 1)
            if i + 2 == ntt - 1 and q + 1 < NQ:
                prefetch_x_dma(q + 1)
                prefetch_l0_weights()
            if i >= 2:
                l1_O(i - 2)
        for i in range(max(ntt - 2, 0), ntt):
            if i == ntt - 1 and q + 1 < NQ:
                prefetch_x_tr(q + 1)
            l1_O(i)

    prefetch_x_dma(0)
    prefetch_l0_weights()
    dma("sp", "d_in", idf, identd, w=["idf"])
    dma("sp", "d_in", trl, trild, w=["x1_2"])
    dma("sp", "d_in", p37[0:31, :], conv_a_w, w=["x1_0"])
    dma("sp", "d_in", p37[31:32, :], conv_a_b, w=["x1_0"])
    dma("sp", "d_in", p37[32:33, :], norm_a_g, w=["x1_0"])
    dma("sp", "d_in", p37[33:34, :], norm_a_b, w=["x1_0"])
    dma("sp", "d_in", p37[34:37, :], conv_b_w, w=["x1_0"])
    for l in range(2):
        dma("sp", "d_in", lng[l], ln_g[l].partition_broadcast(128), w=["lng%d" % l])
        dma("sp", "d_in", lnb[l], ln_b[l].partition_broadcast(128), w=["lnb%d" % l])
    dma("sp", "d_in", wscf, w_s_c.rearrange("h t s -> t h s"), w=["x1_1"])
    dma("sp", "d_in", w00[0:16, :], w_s_c[:, 0, 0].partition_broadcast(16), w=["w00"], allow_slow_non_contiguous=True)
    dma("sp", "d_in", bsf[0:1, 0:1024], b_s_c, w=["x1_3"])

    cp("dve", idb, idf, r=["idf"], w=["idb"])
    memset("dve", ones_b, 1.0, w=["ones"])
    memset("dve", m05, -0.5, w=["m05"])
    memset("dve", a2[:, :, 0:30], 0.0, w=A2ALL)
    memset("dve", ch[:, :, 0:2], 0.0, w=CHALL)
    for j in range(8):
        tr_group([(psum[:, 3584 + j * 37: 3584 + (j + 1) * 37], p37[0:37, j * 128:(j + 1) * 128], idf[0:37, 0:37])],
                 r=["x1_0", "idf"], w=["pb7"])
    cp("dve", prm, psum[:, 3584:3584 + 8 * 37].rearrange("p (j r) -> p j r", j=8), r=["pb7"], w=["prm"])
    ts("dve", prmh, prm[:, :, 0:31], 0.5, None, ALU.mult, None, r=["prm"], w=["prmh"])
    prefetch_x_tr(0)
    for q in range(NQ):
        quarter(q)

    semnames = sorted(P.cnt.keys())
    sems = {n: es.enter_context(nc.semaphore(n)) for n in semnames}
    block = es.enter_context(nc.Block())
    engmap = {"pe": "tensor", "act": "scalar", "dve": "vector", "pool": "gpsimd", "sp": "sync"}

    def make_body(en):
        def body(e):
            waited = {}
            for deps, fn, sem, inc in P.ops[en]:
                for (sn, val) in deps:
                    if en == "pe" and sn == "pe":
                        continue
                    if waited.get(sn, 0) >= val:
                        continue
                    e.wait_ge(sems[sn], val)
                    waited[sn] = val
                ins = fn(e)
                ins.then_inc(sems[sem], inc)
            if en == "sp":
                for sn in semnames:
                    e.wait_ge(sems[sn], P.cnt[sn])
        return body

    for en in ENGS:
        getattr(block, engmap[en])(make_body(en))
    es.close()
    nc.dbg_list = dbg_list
    return nc


_NC_CACHE = {}


def kernel(x_prompt, x_sample, cache_conv_a, cache_conv_b, w_in_ab, conv_a_w, conv_a_b, norm_a_g, norm_a_b,
           conv_b_w, w_out_ab, w_in_c, w_s_c, b_s_c, norm_v_g, norm_v_b, w_out_c, ln_g, ln_b):
    f = lambda a: np.ascontiguousarray(np.asarray(a, dtype=np.float32))
    x_prompt = f(x_prompt); x_sample = f(x_sample)
    cache_conv_a = f(cache_conv_a); cache_conv_b = f(cache_conv_b)
    shared = {
        "w_in_ab": f(w_in_ab)[0], "conv_a_w": f(conv_a_w)[0], "conv_a_b": f(conv_a_b).reshape(1, D),
        "norm_a_g": f(norm_a_g).reshape(1, D), "norm_a_b": f(norm_a_b).reshape(1, D), "conv_b_w": f(conv_b_w)[0],
        "w_out_ab": f(w_out_ab)[0], "w_in_c": f(w_in_c)[0], "w_s_c": f(w_s_c)[0], "b_s_c": f(b_s_c).reshape(1, 1024),
        "norm_v_g": f(norm_v_g).reshape(2048), "norm_v_b": f(norm_v_b).reshape(2048), "w_out_c": f(w_out_c)[0],
        "ln_g": f(ln_g), "ln_b": f(ln_b),
        "identd": np.eye(128, dtype=np.float32), "trild": np.tril(np.ones((128, 128), dtype=np.float32)),
    }
    in_maps = []
    for c in range(NCORES):
        m = dict(shared)
        m["x_p"] = x_prompt[c]
        m["x_s"] = np.ascontiguousarray(x_sample[NS * c:NS * (c + 1), 0, :])
        m["cca"] = np.ascontiguousarray(cache_conv_a[0, NS * c:NS * (c + 1)])
        m["ccb"] = np.ascontiguousarray(cache_conv_b[0, NS * c:NS * (c + 1)])
        in_maps.append(m)
    if "nc" not in _NC_CACHE:
        _NC_CACHE["nc"] = build_program()
    nc = _NC_CACHE["nc"]
    res = run_bass_kernel_spmd(nc, in_maps, core_ids=list(range(NCORES)))
    rs = res.results
    y_prompt = np.stack([rs[c]["y_p"] for c in range(NCORES)], 0)
    y_sample = np.concatenate([rs[c]["y_s"] for c in range(NCORES)], 0)[:, None, :]
    ca_p = np.stack([rs[c]["cap"] for c in range(NCORES)], 0)[None]
    ca_s = np.concatenate([rs[c]["cas"] for c in range(NCORES)], 0)[None]
    cb_p = np.stack([rs[c]["cbp"] for c in range(NCORES)], 0)[None]
    cb_s = np.concatenate([rs[c]["cbs"] for c in range(NCORES)], 0)[None]
    v_s = np.concatenate([rs[c]["vcs"] for c in range(NCORES)], 0)[None, :, None, :]
    return (y_prompt.astype(np.float32), y_sample.astype(np.float32), ca_p.astype(np.float32), ca_s.astype(np.float32),
            cb_p.astype(np.float32), cb_s.astype(np.float32), v_s.astype(np.float32))
```

```python
from contextlib import ExitStack
import numpy as np
import concourse.bass as bass
import concourse.mybir as mybir
from concourse.bass_utils import run_bass_kernel_spmd

F32 = mybir.dt.float32
BF16 = mybir.dt.bfloat16
I32 = mybir.dt.int32
AF = mybir.ActivationFunctionType
ALU = mybir.AluOpType

NCORES = 8
D = 1024
T = 2048
NQ = 4
TQ = 512
NS = 16
WQ = TQ + NS
ALPHA = 4.0 ** 0.25
EPS = 1e-5
KT = 8

ENGS = ["pe", "act", "dve", "pool", "sp"]


class Prog:
    def __init__(self):
        self.ops = {e: [] for e in ENGS}
        self.cnt = {}
        self.lastw = {}
        self.readers = {}
        self.overlap = {}

    def set_overlap(self, a, bs):
        for b in bs:
            self.overlap.setdefault(a, []).append(b)
            self.overlap.setdefault(b, []).append(a)

    def add(self, eng, fn, r=(), w=(), dma=None):
        deps = {}

        def dep(ev):
            if ev is None:
                return
            if deps.get(ev[0], 0) < ev[1]:
                deps[ev[0]] = ev[1]

        w = list(w) + [k for k in r if k.startswith("pb") and k not in w]
        r = [k for k in r if not k.startswith("pb")]
        for k in r:
            dep(self.lastw.get(k))
        for k in w:
            for o in [k] + self.overlap.get(k, []):
                dep(self.lastw.get(o))
                for ev in self.readers.get(o, {}).items():
                    dep(ev)
        if dma:
            sem = ("ld_" + w[0]) if w else ("st_" + (r[0] if r else "misc"))
        else:
            sem = eng
        inc = 16 if dma else 1
        self.cnt[sem] = self.cnt.get(sem, 0) + inc
        ev = (sem, self.cnt[sem])
        self.ops[eng].append((sorted(deps.items()), fn, sem, inc))
        for k in w:
            for o in [k] + self.overlap.get(k, []):
                self.lastw[o] = ev
                self.readers[o] = {}
        for k in r:
            rd = self.readers.setdefault(k, {})
            if rd.get(ev[0], 0) < ev[1]:
                rd[ev[0]] = ev[1]
        return ev


def build_program(debug=False):
    nc = bass.Bass("TRN2", target_bir_lowering=False)
    P = Prog()

    def din(name, shape):
        return nc.dram_tensor(name, list(shape), F32, kind="ExternalInput").ap()

    def dout(name, shape):
        return nc.dram_tensor(name, list(shape), F32, kind="ExternalOutput").ap()

    x_p = din("x_p", [T, D]); x_s = din("x_s", [NS, D])
    cca = din("cca", [NS, 30, D]); ccb = din("ccb", [NS, 2, D])
    w_in_ab = din("w_in_ab", [D, 7168]); conv_a_w = din("conv_a_w", [31, D]); conv_a_b = din("conv_a_b", [1, D])
    norm_a_g = din("norm_a_g", [1, D]); norm_a_b = din("norm_a_b", [1, D]); conv_b_w = din("conv_b_w", [3, D])
    w_out_ab = din("w_out_ab", [2048, D]); w_in_c = din("w_in_c", [D, 6144]); w_s_c = din("w_s_c", [8, 128, 128])
    b_s_c = din("b_s_c", [1, 1024]); norm_v_g = din("norm_v_g", [2048]); norm_v_b = din("norm_v_b", [2048])
    w_out_c = din("w_out_c", [2048, D]); ln_g = din("ln_g", [2, D]); ln_b = din("ln_b", [2, D])
    identd = din("identd", [128, 128]); trild = din("trild", [128, 128])
    y_p = dout("y_p", [T, D]); y_s = dout("y_s", [NS, D]); cap = dout("cap", [30, D]); cas = dout("cas", [NS, 30, D])
    cbp = dout("cbp", [2, D]); cbs = dout("cbs", [NS, 2, D]); vcs = dout("vcs", [NS, 2048])

    es = ExitStack()
    POOLW = 212480 // 4
    pool = es.enter_context(nc.sbuf_tensor("pool", [128, POOLW], F32))
    psum = es.enter_context(nc.psum_tensor("psum", [128, 4096], F32))
    off = [0]

    def alloc(nbytes):
        o = off[0]
        nb = (nbytes + 63) // 64 * 64
        off[0] += nb
        assert off[0] <= POOLW * 4, ("SBUF overflow", off[0])
        return o

    def view(o, nbytes, dt=F32, pat=None, parts=128, **kw):
        ap = pool[0:parts, o // 4:(o + nbytes) // 4]
        if dt is not F32:
            ap = ap.bitcast(dt)
        if pat:
            ap = ap.rearrange(pat, **kw)
        return ap

    def bank(b, n=1):
        return psum[:, b * 512:(b + n) * 512]

    def bank_bf(b):
        return psum[:, b * 512:(b + 1) * 512].bitcast(BF16)

    RA = alloc(81152)
    o = RA
    o_xbf = o; o += 2 * 2048
    o_xtok = o; o += 2 * 4096
    o_xT = o; o += 8448
    o_a2 = o; o += 8928 + 32
    o_ch = o; o += 8480
    o_diag = o; o += 2 * 7936
    o_stat = o; o += 3 * 2112
    o_tmp = o; o += 6 * 2112
    o_sq = o; o += 2 * 1088
    o_tailA = o; o += 1472
    o_tailB = o; o += 576
    o_convs = o; o += 512
    o_smisc = o; o += 512
    assert o - RA <= 81152, o - RA
    o = RA
    o_wv = o; o += 32768
    o_vf = o; o += 8192
    o_vn = o; o += 2 * 4096
    o_o32 = o; o += 2 * 4096
    o_tmpB = o; o += 2 * 2112
    o_nvg = o; o += 8192
    o_nvb = o; o += 8192
    assert o - RA <= 81152, o - RA
    rng0 = {"xT": (o_xT, 8448), "diag0": (o_diag, 7936), "diag1": (o_diag + 7936, 7936),
            "xbf0": (o_xbf, 2048), "xbf1": (o_xbf + 2048, 2048), "xtok0": (o_xtok, 4096), "xtok1": (o_xtok + 4096, 4096),
            "sq0": (o_sq, 1088), "sq1": (o_sq + 1088, 1088), "tailA": (o_tailA, 1472), "tailB": (o_tailB, 576),
            "convs": (o_convs, 512), "sqs": (o_smisc, 512)}
    for j in range(8):
        rng0["a2_%d" % j] = (o_a2 + 1116 * j, 1116)
        rng0["ch_%d" % j] = (o_ch + 1060 * j, 1060)
    for i in range(3):
        rng0["stat%d" % i] = (o_stat + 2112 * i, 2112)
    for i in range(6):
        rng0["tmp%d" % i] = (o_tmp + 2112 * i, 2112)
    rng1 = {"vf": (o_vf, 8192), "vn0": (o_vn, 4096), "vn1": (o_vn + 4096, 4096),
            "o320": (o_o32, 4096), "o321": (o_o32 + 4096, 4096), "tmpB0": (o_tmpB, 2112), "tmpB1": (o_tmpB + 2112, 2112),
            "nvg": (o_nvg, 8192), "nvb": (o_nvb, 8192)}
    rng0["xin2"] = (o_xtok, 2048)
    rng0["xin3"] = (o_xtok + 2048, 2048)
    P.set_overlap("xtok0", ["xin2", "xin3"])
    for c in range(8):
        rng1["wv%d" % c] = (o_wv + 4096 * c, 4096)
    for k0, (a0, n0) in rng0.items():
        ov = [k1 for k1, (a1, n1) in rng1.items() if a0 < a1 + n1 and a1 < a0 + n0]
        if ov:
            P.set_overlap(k0, ov)

    xT = view(o_xT, 8448, BF16, "p (k t) -> p k t", k=8)
    a2 = view(o_a2, 8928, BF16, "p (k t) -> p k t", k=8)
    ch = view(o_ch, 8480, BF16, "p (k t) -> p k t", k=8)
    diag = [view(o_diag + i * 7936, 7936, BF16, "p (k t) -> p k t", k=31) for i in range(2)]
    stat = [view(o_stat + i * 2112, 2112) for i in range(3)]
    xtok = [view(o_xtok + i * 4096, 4096) for i in range(2)]
    tmp = [view(o_tmp + i * 2112, 2112) for i in range(6)]
    xbf = [view(o_xbf + i * 2048, 2048, BF16) for i in range(2)]
    xin = [(xbf[0], "xbf0"), (xbf[1], "xbf1"), (view(o_xtok, 2048, BF16), "xin2"), (view(o_xtok + 2048, 2048, BF16), "xin3")]
    sq = [view(o_sq + i * 1088, 1056, BF16) for i in range(2)]
    tailA = view(o_tailA, 8 * 46 * 4, F32, "p (j t) -> p j t", j=8)
    tailB = view(o_tailB, 8 * 18 * 4, F32, "p (j t) -> p j t", j=8)
    convs = view(o_convs, 512, F32, "p (j t) -> p j t", j=8)
    smisc = view(o_smisc, 512)
    wvc = view(o_wv, 32768, BF16, "p (c k e) -> p c k e", c=8, k=8)
    WVALL = ["wv%d" % c for c in range(8)]
    vf = view(o_vf, 8192)
    vfs = [(vf, "vf"), (view(o_nvb, 8192), "nvb")]
    vn = [view(o_vn + i * 4096, 4096, BF16) for i in range(2)]
    o32 = [view(o_o32 + i * 4096, 4096) for i in range(2)]
    tmpB = [view(o_tmpB + i * 2112, 2112) for i in range(2)]
    nvg = view(o_nvg, 8192)
    nvb = view(o_nvb, 8192)

    yb = view(alloc(16896), 16896, BF16, "p (e t) -> p e t", e=16)
    x1T = view(alloc(8448), 8448, BF16, "p (k t) -> p k t", k=8)
    o_x1 = alloc(5 * 4096)
    x1 = view(o_x1, 5 * 4096, F32, "p (i d) -> p i d", i=5)
    scc = [view(o_x1 + i * 4096, 4096) for i in range(2)]
    swr = view(o_x1 + 2 * 4096, 4096)
    spr = view(o_x1 + 3 * 4096, 4096)
    tailT = view(o_x1 + 4 * 4096, 4096)
    p37 = view(o_x1, 4096)
    wscf = view(o_x1 + 4096, 4096, F32, "p (h s) -> p h s", h=8)
    wscm = view(o_x1 + 2 * 4096, 2048, BF16, "p (h s) -> p h s", h=8)
    trl = view(o_x1 + 2 * 4096 + 2048, 512)
    bsf = view(o_x1 + 3 * 4096, 4096)
    bsh32 = view(o_x1 + 4 * 4096, 4096)
    NWS = 4
    GW = 256
    wst = [view(alloc(4096), 4096, BF16, "p (k e) -> p k e", k=8) for _ in range(NWS)]
    wres = view(alloc(32768), 32768, BF16, "p (e d) -> p e d", e=16)
    lng = [view(alloc(4096), 4096) for _ in range(2)]
    lnb = [view(alloc(4096), 4096) for _ in range(2)]
    prm = view(alloc(8 * 37 * 4), 8 * 37 * 4, F32, "p (j r) -> p j r", j=8)
    prmh = view(alloc(8 * 31 * 4), 8 * 31 * 4, F32, "p (j r) -> p j r", j=8)
    idf = view(alloc(512), 512)
    idb = view(alloc(256), 256, BF16)
    R = view(alloc(2048), 2048, BF16, "p (h t) -> p h t", h=8)
    Rs = view(alloc(256), 256, BF16, "p (h t) -> p h t", h=8)
    diagB = view(alloc(8 * 3 * 256), 8 * 3 * 256, BF16, "p (j k t) -> p j k t", j=8, k=3)
    ones_b = view(alloc(256), 256, BF16)
    onesrow = ones_b[0:1, :]
    w00 = view(alloc(64), 32)
    bsh = view(o_tmp, 2048, BF16)
    bsl = view(o_tmp + 2112, 2048, BF16)
    bssh = view(o_tmp + 3 * 2112, 256, BF16, "p (h t) -> p h t", h=8)
    bssl = view(o_tmp + 3 * 2112 + 256, 256, BF16, "p (h t) -> p h t", h=8)
    rss = view(o_tmp + 3 * 2112 + 512, 256, BF16, "p (h t) -> p h t", h=8)
    rsh = view(o_tmp + 2 * 2112, 2048, BF16)
    L4 = view(alloc(4096), 4096, BF16)
    R4 = view(alloc(2048), 2048, BF16, "p (h t) -> p h t", h=8)
    R4s = view(alloc(256), 256, BF16, "p (h t) -> p h t", h=8)
    mv = [view(alloc(64), 16 * 4) for _ in range(2)]
    bnst = [view(alloc(64), 6 * 4 * 2) for _ in range(2)]
    bnst = [view(alloc(128), 24 * 4) for _ in range(2)]
    m05 = view(alloc(64), 4)
    halo_a = view(alloc(8 * 30 * 2), 8 * 30 * 2, BF16, "p (j t) -> p j t", j=8)
    halo_c = view(alloc(64), 8 * 2 * 2, BF16, "p (j t) -> p j t", j=8)
    print("SBUF bytes used per partition:", off[0], "of", POOLW * 4)
    indA = view(alloc(64), 16)
    indB = view(alloc(64), 64)

    def yk(e):
        return "y%d" % e
    YALL = ["y%d" % e for e in range(16)]
    A2ALL = ["a2_%d" % j for j in range(8)]
    CHALL = ["ch_%d" % j for j in range(8)]

    def dma(eng, stream, out, in_, r=(), w=(), **kw):
        return P.add(eng, lambda e: e.dma_start(out=out, in_=in_, **kw), r=r, w=w, dma=stream)

    def act(out, in_, func, r, w, bias=0.0, scale=1.0):
        return P.add("act", lambda e: e.activation(out=out, in_=in_, func=func, bias=bias, scale=scale), r=r, w=w)

    def tt(eng, out, in0, in1, op, r, w):
        return P.add(eng, lambda e: e.tensor_tensor(out=out, in0=in0, in1=in1, op=op), r=r, w=w)

    def ts(eng, out, in0, s1, s2, op0, op1, r, w):
        if s2 is None:
            return P.add(eng, lambda e: e.tensor_scalar(out=out, in0=in0, scalar1=s1, scalar2=None, op0=op0), r=r, w=w)
        return P.add(eng, lambda e: e.tensor_scalar(out=out, in0=in0, scalar1=s1, scalar2=s2, op0=op0, op1=op1), r=r, w=w)

    def stt(out, in0, scalar, in1, op0, op1, r, w):
        return P.add("dve", lambda e: e.scalar_tensor_tensor(out=out, in0=in0, scalar=scalar, in1=in1, op0=op0, op1=op1), r=r, w=w)

    def cp(eng, out, in_, r, w):
        if eng == "act":
            return P.add(eng, lambda e: e.activation(out=out, in_=in_, func=AF.Copy), r=r, w=w)
        return P.add(eng, lambda e: e.tensor_copy(out=out, in_=in_), r=r, w=w)

    def memset(eng, ap, val, w):
        return P.add(eng, lambda e: e.memset(ap, val), w=w)

    def mm_group(out, pairs, r, w, first=True, last=True):
        def fn(e):
            n = len(pairs)
            ins = None
            for i, (l, rr) in enumerate(pairs):
                ins = e.matmul(out, lhsT=l, rhs=rr, start=(first and i == 0), stop=(last and i == n - 1))
            return ins
        return P.add("pe", fn, r=r, w=w)

    def tr_group(items, r, w):
        def fn(e):
            ins = None
            for (o_, i_, id_) in items:
                ins = e.transpose(out=o_, in_=i_, identity=id_)
            return ins
        return P.add("pe", fn, r=r, w=w)

    dbg_list = []

    def dbg(name, ap, keys, dt=F32):
        if not debug:
            return
        d = nc.dram_tensor("dbg_" + name, list(ap.shape), dt, kind="ExternalOutput").ap()
        dma("sp", "d_out", d, ap, r=keys)
        dbg_list.append("dbg_" + name)

    wst_i = [0]

    pending = []

    def load_wgroup(wsrc, c0):
        s = wst_i[0] % NWS
        wst_i[0] += 1
        dma("pool", "d_w", wst[s], wsrc[:, c0:c0 + GW].rearrange("(k p) e -> p k e", p=128), w=["wst%d" % s])
        if pending:
            pending.pop(0)()
        return s

    def push_wres(wsrc):
        for c in range(4):
            pending.append(lambda c=c: dma("pool", "d_w", wres[:, 4 * c:4 * c + 4, :],
                                           wsrc[512 * c:512 * (c + 1), :].rearrange("(e p) d -> p e d", p=128), w=["wres"]))

    def load_wv_chunk(c):
        dma("pool", "d_w", wvc[:, c, :, :],
            w_in_c[:, 2048 + 256 * c:2048 + 256 * (c + 1)].rearrange("(k p) e -> p k e", p=128), w=["wv%d" % c])

    def push_wv():
        for c in range(3):
            pending.append(lambda c=c: load_wv_chunk(c))

    def flush_pending():
        while pending:
            pending.pop(0)()

    def segs(q):
        sg = [(0, TQ)]
        if q == NQ - 1:
            sg.append((TQ, NS))
        return sg

    pi_i = [0]

    def inproj(src, srckey, wslot, et, q, consumer, nbank=3):
        for si, (c0, n) in enumerate(segs(q)):
            if si == 0:
                ring = [0, 1, 2, 7] if (nbank == 3 and q < NQ - 1) else list(range(nbank))
                b = ring[pi_i[0] % len(ring)]
                pi_i[0] += 1
                pap = bank(b)[:, 0:n]
                pkey = "pb%d" % b
            else:
                pap = psum[:, 3584 + 256:3584 + 256 + n]
                pkey = "pb7"
            mm_group(pap, [(wst[wslot][:, k, et * 128:(et + 1) * 128], src[:, k, c0:c0 + n]) for k in range(KT)],
                     r=["wst%d" % wslot, srckey], w=[pkey])
            consumer(si, c0, n, pap, pkey)

    tmp_i = [0]

    def newtmp():
        i = tmp_i[0] % 6
        tmp_i[0] += 1
        return i

    def dve_rsqrt(mvt, np_, cx, cy, cc, key):
        x = mvt[0:np_, cx:cx + 1]
        y = mvt[0:np_, cy:cy + 1]
        c = mvt[0:np_, cc:cc + 1]
        P.add("dve", lambda e: e.tensor_single_scalar(out=y.bitcast(I32), in_=x.bitcast(I32), scalar=1, op=ALU.arith_shift_right),
              r=[key], w=[key])
        P.add("dve", lambda e: e.tensor_scalar(out=y.bitcast(I32), in0=y.bitcast(I32), scalar1=-1.0, scalar2=float(0x5f3759df),
                                               op0=ALU.mult, op1=ALU.add), r=[key], w=[key])
        for _ in range(2):
            stt(c, y, x, y, ALU.mult, ALU.mult, r=[key], w=[key])
            ts("dve", c, c, -0.5, 1.5, ALU.mult, ALU.add, r=[key], w=[key])
            tt("dve", y, y, c, ALU.mult, r=[key], w=[key])

    aff_pending = []

    def resid_ln(l, i, np_, xres, xres_keys, po, pokeys, outbuf, outkey, bfout=None, bfkey=None):
        mi = i % 2
        stt(outbuf, xres, ALPHA, po, ALU.mult, ALU.add, r=xres_keys + pokeys, w=[outkey])
        for hh in range(2):
            P.add("dve", lambda e, hh=hh: e.bn_stats(out=bnst[mi][0:np_, hh * 6:(hh + 1) * 6], in_=outbuf[:, hh * 512:(hh + 1) * 512]),
                  r=[outkey], w=["bnst%d" % mi])
        P.add("dve", lambda e: e.bn_aggr(out=mv[mi][0:np_, 0:2], in_=bnst[mi][0:np_, 0:12]), r=["bnst%d" % mi], w=["mv%d" % mi])
        ts("dve", mv[mi][0:np_, 2:3], mv[mi][0:np_, 1:2], EPS, None, ALU.add, None, r=["mv%d" % mi], w=["mv%d" % mi])
        dve_rsqrt(mv[mi], np_, 2, 3, 10, "mv%d" % mi)
        ts("dve", outbuf, outbuf, mv[mi][0:np_, 0:1], mv[mi][0:np_, 3:4], ALU.subtract, ALU.mult, r=[outkey, "mv%d" % mi], w=[outkey])
        tt("dve" if l == 0 else "pool", outbuf, outbuf, lng[l][0:np_, :], ALU.mult, r=[outkey, "lng%d" % l], w=[outkey])
        if bfout is not None:
            tt("dve", bfout, outbuf, lnb[l][0:np_, :], ALU.add, r=[outkey, "lnb%d" % l], w=[bfkey])
        if bfout is not None:
            aff_pending.append(lambda: tt("pool", outbuf, outbuf, lnb[l][0:np_, :], ALU.add, r=[outkey, "lnb%d" % l], w=[outkey]))
        else:
            tt("pool", outbuf, outbuf, lnb[l][0:np_, :], ALU.add, r=[outkey, "lnb%d" % l], w=[outkey])

    def prefetch_x_dma(q):
        ntt_ = 5 if q == NQ - 1 else 4
        if q > 0:
            cp("pool", a2[:, :, 0:30], halo_a, r=["halo_a"], w=A2ALL)
        xall = view(o_xbf, 4 * 2048, BF16, "p (i d) -> p i d", i=4)
        dma("pool", "d_x", xall, x_p[q * TQ:(q + 1) * TQ, :].rearrange("(i p) d -> p i d", p=128),
            w=["xbf0", "xbf1", "xin2", "xin3"])

    def prefetch_x_tr(q):
        ntt_ = 5 if q == NQ - 1 else 4
        for i in range(ntt_):
            buf, key = xin[i % 4]
            np_ = 128 if i < 4 else NS
            if i == 4:
                dma("pool", "d_x", buf[0:np_, :], x_s, w=[key])
            pst = bank_bf(0)[:, 0:8 * np_].rearrange("p (k t) -> p k t", k=8)
            tr_group([(pst[:, k, :], buf[0:np_, k * 128:(k + 1) * 128], idb[0:np_, 0:np_]) for k in range(8)],
                     r=[key, "idb"], w=["pb0"])
            cp("act", xT[:, :, i * 128:i * 128 + np_], pst, r=["pb0"], w=["xT"])

    def prologue_deferred():
        for j in range(8):
            tt("dve", diagB[:, j, :, :], idb.unsqueeze(1).to_broadcast([128, 3, 128]),
               prm[:, j, 34:37].unsqueeze(2).to_broadcast([128, 3, 128]), ALU.mult, r=["idb", "prm"], w=["diagB"])
        for h in range(8):
            tt("dve", wscm[:, h, :], wscf[:, h, :], trl, ALU.mult, r=["x1_1", "x1_2"], w=["x1_2"])
        tr_group([(bank_bf(6)[:, h * 128:(h + 1) * 128], wscm[:, h, :], idb) for h in range(8)], r=["x1_2", "idb"], w=["pb6"])
        cp("dve", R, bank_bf(6)[:, 0:1024].rearrange("p (h t) -> p h t", h=8), r=["pb6"], w=["R"])
        for h in range(8):
            ts("dve", Rs[0:16, h, :], idf[0:16, 0:16], w00[0:16, h:h + 1], None, ALU.mult, None, r=["idf", "w00"], w=["Rs"])
        cp("dve", bsh[0:1, 0:1024], bsf[0:1, 0:1024], r=["x1_3"], w=["tmp0"])
        cp("dve", bsh32[0:1, 0:1024], bsh[0:1, 0:1024], r=["tmp0"], w=["x1_4"])
        tt("dve", bsl[0:1, 0:1024], bsf[0:1, 0:1024], bsh32[0:1, 0:1024], ALU.subtract, r=["x1_3", "x1_4"], w=["tmp1"])
        bsh3 = bsh[0:1, 0:1024].rearrange("p (h t) -> p h t", h=8)
        bsl3 = bsl[0:1, 0:1024].rearrange("p (h t) -> p h t", h=8)
        for h in range(8):
            cp("dve", bssh[0:1, h, :], bsh3[0:1, h, 0:1].to_broadcast([1, 16]), r=["tmp0"], w=["tmp3"])
            cp("dve", bssl[0:1, h, :], bsl3[0:1, h, 0:1].to_broadcast([1, 16]), r=["tmp1"], w=["tmp3"])
        mm_group(psum[0:1, 0:512], [(ones_b[:, 0:1], R[:, 0:4, :])], r=["ones", "R"], w=["pb0"])
        mm_group(psum[0:1, 512:1024], [(ones_b[:, 0:1], R[:, 4:8, :])], r=["ones", "R"], w=["pb1"])
        cp("dve", rsh[0:1, 0:1024], psum[0:1, 0:1024], r=["pb0", "pb1"], w=["tmp2"])
        mm_group(psum[0:1, 1024:1152], [(ones_b[0:16, 0:1], Rs[0:16, :, :])], r=["ones", "Rs"], w=["pb2"])
        cp("dve", rss[0:1, :, :], psum[0:1, 1024:1152].rearrange("p (h t) -> p h t", h=8), r=["pb2"], w=["tmp3"])
        nb32 = pool[0:1, o_x1 // 4:o_x1 // 4 + 2048]
        nbh = pool[0:1, (o_x1 + 8192) // 4:(o_x1 + 12288) // 4].bitcast(BF16)
        nbh32 = pool[0:1, (o_x1 + 12288) // 4:(o_x1 + 20480) // 4]
        nbl = pool[0:1, (o_x1 + 12288) // 4:(o_x1 + 16384) // 4].bitcast(BF16)
        dma("sp", "d_in", nb32, norm_v_b.rearrange("(o n) -> o n", o=1), r=[], w=["x1_0", "x1_1"])
        cp("dve", nbh, nb32, r=["x1_0", "x1_1"], w=["x1_2"])
        cp("dve", nbh32, nbh, r=["x1_2"], w=["x1_3", "x1_4"])
        tt("dve", nb32, nb32, nbh32, ALU.subtract, r=["x1_0", "x1_1", "x1_3", "x1_4"], w=["x1_0", "x1_1"])
        cp("dve", nbl, nb32, r=["x1_0", "x1_1"], w=["x1_3"])
        memset("dve", L4, 0.0, w=["L4"])
        memset("dve", R4, 0.0, w=["R4"])
        memset("dve", R4s, 0.0, w=["R4s"])
        memset("dve", L4[0:2, :], 1.0, w=["L4"])
        dma("sp", "d_in", L4[2:3, :], nbh, r=["x1_2"], w=["L4"])
        dma("sp", "d_in", L4[3:4, :], nbl, r=["x1_3"], w=["L4"])
        dma("sp", "d_in", R4[0:1, :, :], bsh3, r=["tmp0"], w=["R4"])
        dma("sp", "d_in", R4[1:2, :, :], bsl3, r=["tmp1"], w=["R4"])
        dma("sp", "d_in", R4[2:3, :, :], rsh[0:1, 0:1024].rearrange("p (h t) -> p h t", h=8), r=["tmp2"], w=["R4"])
        dma("sp", "d_in", R4[3:4, :, :], rsh[0:1, 0:1024].rearrange("p (h t) -> p h t", h=8), r=["tmp2"], w=["R4"])
        dma("sp", "d_in", R4s[0:1, :, :], bssh[0:1, :, :], r=["tmp3"], w=["R4s"])
        dma("sp", "d_in", R4s[1:2, :, :], bssl[0:1, :, :], r=["tmp3"], w=["R4s"])
        dma("sp", "d_in", R4s[2:3, :, :], rss[0:1, :, :], r=["tmp3"], w=["R4s"])
        dma("sp", "d_in", R4s[3:4, :, :], rss[0:1, :, :], r=["tmp3"], w=["R4s"])
        memset("dve", indA, 0.0, w=["indA"])
        for b in range(4):
            memset("dve", indA[32 * b:32 * b + 32, b:b + 1], 1.0, w=["indA"])
        dma("sp", "d_in", indB[0:16, :], identd[0:16, 0:16], w=["indB"])
        dma("sp", "d_in", indB[16:32, :], identd[0:16, 0:16], w=["indB"])


    next_pre = {}

    def prefetch_l0_weights():
        next_pre[("g", 0)] = load_wgroup(w_in_ab, 1024 + 0 * GW)
        next_pre[("v", 0)] = load_wgroup(w_in_ab, 0 + 0 * GW)
        next_pre[("g", 1)] = load_wgroup(w_in_ab, 1024 + 1 * GW)
        next_pre[("v", 1)] = load_wgroup(w_in_ab, 0 + 1 * GW)

    def quarter(q):
        t0 = q * TQ
        ntt = 5 if q == NQ - 1 else 4
        pre = dict(next_pre)
        next_pre.clear()
        if q > 0:
            cp("pool", ch[:, :, 0:2], halo_c, r=["halo_c"], w=CHALL)
        if q == 0:
            dbg("xT", xT, ["xT"], BF16)
            dbg("prm", prm, ["prm"])
            dbg("R", R, ["R"], BF16)

        NE = GW // 128

        def sec_gate_val(pr):
            sg = pre.pop(("g", pr)) if ("g", pr) in pre else load_wgroup(w_in_ab, 1024 + pr * GW)
            sv = pre.pop(("v", pr)) if ("v", pr) in pre else load_wgroup(w_in_ab, 0 + pr * GW)
            gt = {}
            for e in range(NE):
                j = pr * NE + e
                ti = newtmp()
                gt[j] = ti

                def cons_gate(si, c0, n, pap, pkey, ti=ti):
                    act(tmp[ti][:, c0:c0 + n], pap, AF.Tanh, r=[pkey], w=["tmp%d" % ti], scale=0.5)
                inproj(xT, "xT", sg, e, q, cons_gate)
            for e in range(NE):
                j = pr * NE + e
                ti = gt[j]

                def cons_val(si, c0, n, pap, pkey, ti=ti, j=j):
                    stt(a2[:, j, 30 + c0:30 + c0 + n], tmp[ti][:, c0:c0 + n], 1.0, pap, ALU.add, ALU.mult,
                        r=[pkey, "tmp%d" % ti], w=["a2_%d" % j])
                    if q == NQ - 1:
                        if si == 0:
                            stt(tailA[:, j, 0:30], tmp[ti][:, 482:512], 1.0, pap[:, 482:512], ALU.add, ALU.mult,
                                r=[pkey, "tmp%d" % ti], w=["tailA"])
                        else:
                            stt(tailA[:, j, 30:46], tmp[ti][:, c0:c0 + n], 1.0, pap, ALU.add, ALU.mult,
                                r=[pkey, "tmp%d" % ti], w=["tailA"])
                inproj(xT, "xT", sv, e, q, cons_val)

        conv_i = [0]
        diag_i = [0]

        def build_diag(j):
            s = diag_i[0] % 2
            diag_i[0] += 1
            tt("dve", diag[s], idb.unsqueeze(1).to_broadcast([128, 31, 128]),
               prmh[:, j, :].unsqueeze(2).to_broadcast([128, 31, 128]), ALU.mult, r=["idb", "prmh"], w=["diag%d" % s])
            return s

        def conv_a(j, ds):
            b = 3 + conv_i[0] % 2
            conv_i[0] += 1
            mm_group(bank(b), [(diag[ds][:, k, :], a2[:, j, k:k + TQ]) for k in range(31)],
                     r=["diag%d" % ds, "a2_%d" % j], w=["pb%d" % b])
            act(yb[:, j, 0:TQ], bank(b), AF.Identity, r=["pb%d" % b, "prm"], w=[yk(j)], bias=prm[:, j, 31:32])
            s = j % 2
            act(sq[s][:, 0:TQ], yb[:, j, 0:TQ], AF.Square, r=[yk(j)], w=["sq%d" % s])
            P.add("pe", lambda e, j=j: e.matmul(bank(5), lhsT=ones_b, rhs=yb[:, j, 0:TQ], start=(j == 0), stop=(j == 7)),
                  r=[yk(j), "ones"], w=["pb5"])
            P.add("pe", lambda e, j=j, s=s: e.matmul(bank(6), lhsT=ones_b, rhs=sq[s][:, 0:TQ], start=(j == 0), stop=(j == 7)),
                  r=["sq%d" % s, "ones"], w=["pb6"])

        def sec_cpre_h(pr):
            sc = load_wgroup(w_in_ab, 3072 + pr * GW)
            sh = load_wgroup(w_in_ab, 4096 + pr * GW)
            ctmp = {}
            for e in range(NE):
                j = pr * NE + e
                ti = newtmp()
                ctmp[j] = ti

                def cons_c(si, c0, n, pap, pkey, ti=ti):
                    act(tmp[ti][:, c0:c0 + n], pap, AF.Copy, r=[pkey], w=["tmp%d" % ti])
                inproj(xT, "xT", sc, e, q, cons_c)
            for e in range(NE):
                j = pr * NE + e
                ti = ctmp[j]

                def cons_h(si, c0, n, pap, pkey, ti=ti, j=j):
                    tt("dve", ch[:, j, 2 + c0:2 + c0 + n], tmp[ti][:, c0:c0 + n], pap, ALU.mult,
                       r=[pkey, "tmp%d" % ti], w=["ch_%d" % j])
                    if q == NQ - 1:
                        if si == 0:
                            tt("dve", tailB[:, j, 0:2], tmp[ti][:, 510:512], pap[:, 510:512], ALU.mult,
                               r=[pkey, "tmp%d" % ti], w=["tailB"])
                        else:
                            tt("dve", tailB[:, j, 2:18], tmp[ti][:, c0:c0 + n], pap, ALU.mult,
                               r=[pkey, "tmp%d" % ti], w=["tailB"])
                inproj(xT, "xT", sh, e, q, cons_h)

        def sec_b_rest(pr):
            sp_ = load_wgroup(w_in_ab, 2048 + pr * GW)
            sgb = load_wgroup(w_in_ab, 6144 + pr * GW)
            s_tmp = {}
            for e in range(NE):
                j = pr * NE + e
                ti = newtmp()
                s_tmp[j] = ti
                b = 3 + conv_i[0] % 2
                conv_i[0] += 1
                mm_group(bank(b), [(diagB[:, j, k, :], ch[:, j, k:k + TQ]) for k in range(3)],
                         r=["diagB", "ch_%d" % j], w=["pb%d" % b])
                act(tmp[ti][:, 0:TQ], bank(b), AF.Copy, r=["pb%d" % b], w=["tmp%d" % ti])
                if q == NQ - 1:
                    stt(tmp[ti][:, TQ:WQ], tailB[:, j, 2:18], prm[:, j, 36:37], psum[:, 3584 + 384 + j * 16:3584 + 384 + (j + 1) * 16],
                        ALU.mult, ALU.add, r=["tailB", "prm", "pb7"], w=["tmp%d" % ti])
            for e in range(NE):
                j = pr * NE + e
                ti = s_tmp[j]

                def cons_bp(si, c0, n, pap, pkey, ti=ti):
                    tt("dve", tmp[ti][:, c0:c0 + n], tmp[ti][:, c0:c0 + n], pap, ALU.mult,
                       r=[pkey, "tmp%d" % ti], w=["tmp%d" % ti])
                inproj(xT, "xT", sp_, e, q, cons_bp)
            for e in range(NE):
                j = pr * NE + e
                ti = s_tmp[j]
                t2 = newtmp()

                def cons_gb(si, c0, n, pap, pkey, ti=ti, t2=t2, j=j):
                    act(tmp[t2][:, c0:c0 + n], pap, AF.Silu, r=[pkey], w=["tmp%d" % t2])
                    tt("dve", yb[:, 8 + j, c0:c0 + n], tmp[ti][:, c0:c0 + n], tmp[t2][:, c0:c0 + n], ALU.mult,
                       r=["tmp%d" % ti, "tmp%d" % t2], w=[yk(8 + j)])
                inproj(xT, "xT", sgb, e, q, cons_gb)

        def sec_ga(pr):
            sga = load_wgroup(w_in_ab, 5120 + pr * GW)
            for e in range(NE):
                j = pr * NE + e
                t1 = newtmp()
                t2 = newtmp()

                def cons_ga(si, c0, n, pap, pkey, t1=t1, t2=t2, j=j):
                    act(tmp[t1][:, c0:c0 + n], pap, AF.Silu, r=[pkey], w=["tmp%d" % t1])
                    tt("dve", tmp[t2][:, c0:c0 + n], yb[:, j, c0:c0 + n], stat[0][:, c0:c0 + n], ALU.subtract,
                       r=[yk(j), "stat0"], w=["tmp%d" % t2])
                    tt("dve", tmp[t2][:, c0:c0 + n], tmp[t2][:, c0:c0 + n], stat[1][:, c0:c0 + n], ALU.mult,
                       r=["tmp%d" % t2, "stat1"], w=["tmp%d" % t2])
                    act(tmp[t2][:, c0:c0 + n], tmp[t2][:, c0:c0 + n], AF.Silu, r=["tmp%d" % t2, "prm"], w=["tmp%d" % t2],
                        bias=prm[:, j, 33:34], scale=prm[:, j, 32:33])
                    tt("dve", yb[:, j, c0:c0 + n], tmp[t2][:, c0:c0 + n], tmp[t1][:, c0:c0 + n], ALU.mult,
                       r=["tmp%d" % t1, "tmp%d" % t2], w=[yk(j)])
                inproj(xT, "xT", sga, e, q, cons_ga)

        sprs = [(spr, "x1_3"), (tailT, "x1_4")]

        def samp_load(gi):
            if gi == 0:
                memset("pool", swr, 0.0, w=["x1_2"])
                for bb in range(4):
                    dma("sp", "d_in", swr[32 * bb:32 * bb + 30, 0:1024], conv_a_w[0:30, :], w=["x1_2"])
            s = gi % 2
            sp_, spk = sprs[gi % 2]
            memset("pool", scc[s], 0.0, w=["x1_%d" % s])
            for bb in range(4):
                dma("sp", "d_in", scc[s][32 * bb:32 * bb + 30, 0:1024], cca[4 * gi + bb, :, :], w=["x1_%d" % s])
            tt("dve", sp_, scc[s], swr, ALU.mult, r=["x1_%d" % s, "x1_2"], w=[spk])

        def samp_mm(gi):
            sp_, spk = sprs[gi % 2]
            for j in range(8):
                c = 3584 + 128 + j * 16 + gi * 4
                P.add("pe", lambda e, c=c, j=j: e.matmul(psum[:, c:c + 4], lhsT=sp_[:, j * 128:(j + 1) * 128], rhs=indA,
                                                        start=True, stop=True),
                      r=[spk, "indA"], w=["pb7"])

        def samp_fin_a():
            ts("dve", tailA, tailA, 0.5, None, ALU.mult, None, r=["tailA"], w=["tailA"])
            for j in range(8):
                stt(convs[:, j, :], tailA[:, j, 30:46], prm[:, j, 30:31], psum[:, 3584 + 128 + j * 16:3584 + 128 + (j + 1) * 16],
                    ALU.mult, ALU.add, r=["tailA", "prm", "pb7"], w=["convs"])
                ts("dve", yb[:, j, TQ:WQ], convs[:, j, :], prm[:, j, 31:32], None, ALU.add, None, r=["convs", "prm"], w=[yk(j)])

        def samp_load_b():
            dma("sp", "d_in", scc[0][0:16, 0:1024], ccb[:, 0, :], w=["x1_0"])
            dma("sp", "d_in", scc[0][16:32, 0:1024], ccb[:, 1, :], w=["x1_0"])
            dma("sp", "d_in", scc[1][0:16, 0:1024], conv_b_w[0].partition_broadcast(16), w=["x1_1"])
            dma("sp", "d_in", scc[1][16:32, 0:1024], conv_b_w[1].partition_broadcast(16), w=["x1_1"])
            tt("dve", spr[0:32, :], scc[0][0:32, :], scc[1][0:32, :], ALU.mult, r=["x1_0", "x1_1"], w=["x1_3"])

        def samp_mm_b():
            for j in range(8):
                c = 3584 + 384 + j * 16
                P.add("pe", lambda e, c=c, j=j: e.matmul(psum[:, c:c + 16], lhsT=spr[0:32, j * 128:(j + 1) * 128], rhs=indB[0:32, :],
                                                         start=True, stop=True),
                      r=["x1_3", "indB"], w=["pb7"])
            dma("sp", "d_out", cas[:, 0:29, :], cca[:, 1:30, :])
            dma("sp", "d_out", cbs[:, 0, :], ccb[:, 1, :])

        def tails_out():
            pt = psum[0:46, 0:1024]
            tr_group([(pt[:, j * 128:(j + 1) * 128], tailA[:, j, :], idf) for j in range(8)], r=["tailA", "idf"], w=["pb0", "pb1"])
            cp("dve", tailT[0:46, :], pt, r=["pb0", "pb1"], w=["x1_4"])
            dma("sp", "d_out", cap, tailT[0:30, :], r=["x1_4"])
            dma("sp", "d_out", cas[:, 29, :], tailT[30:46, :], r=["x1_4"])
            pt2 = psum[0:18, 0:1024]
            tr_group([(pt2[:, j * 128:(j + 1) * 128], tailB[:, j, :], idf) for j in range(8)], r=["tailB", "idf"], w=["pb0", "pb1"])
            cp("dve", tailT[0:18, :], pt2, r=["pb0", "pb1"], w=["x1_4"])
            dma("sp", "d_out", cbp, tailT[0:2, :], r=["x1_4"])
            dma("sp", "d_out", cbs[:, 1, :], tailT[2:18, :], r=["x1_4"])

        def ln_a_stats():
            if q == NQ - 1:
                sqs = smisc.bitcast(BF16).rearrange("p (j t) -> p j t", j=8)[:, :, 0:NS]
                for j in range(8):
                    tt("dve", sqs[:, j, :], yb[:, j, TQ:WQ], yb[:, j, TQ:WQ], ALU.mult, r=[yk(j)], w=["sqs"])
                mm_group(psum[:, 3584 + 96:3584 + 112], [(ones_b, yb[:, j, TQ:WQ]) for j in range(8)],
                         r=[yk(j) for j in range(8)] + ["ones"], w=["pb7"])
                mm_group(psum[:, 3584 + 112:3584 + 128], [(ones_b, sqs[:, j, :]) for j in range(8)],
                         r=["sqs", "ones"], w=["pb7"])
            for (c0, n) in segs(q):
                p1 = bank(5)[:, 0:n] if c0 == 0 else psum[:, 3584 + 96:3584 + 112]
                p2 = bank(6)[:, 0:n] if c0 == 0 else psum[:, 3584 + 112:3584 + 128]
                k1 = ["pb5"] if c0 == 0 else ["pb7"]
                k2 = ["pb6"] if c0 == 0 else ["pb7"]
                ts("dve", stat[0][:, c0:c0 + n], p1, 1.0 / D, None, ALU.mult, None, r=k1, w=["stat0"])
                tt("dve", stat[2][:, c0:c0 + n], stat[0][:, c0:c0 + n], stat[0][:, c0:c0 + n], ALU.mult, r=["stat0"], w=["stat2"])
                stt(stat[1][:, c0:c0 + n], p2, 1.0 / D, stat[2][:, c0:c0 + n], ALU.mult, ALU.subtract, r=k2 + ["stat2"], w=["stat1"])
                ts("dve", stat[1][:, c0:c0 + n], stat[1][:, c0:c0 + n], EPS, None, ALU.add, None, r=["stat1"], w=["stat1"])
                act(stat[1][:, c0:c0 + n], stat[1][:, c0:c0 + n], AF.Sqrt, r=["stat1"], w=["stat1"])
                P.add("dve", lambda e, c0=c0, n=n: e.reciprocal(out=stat[1][:, c0:c0 + n], in_=stat[1][:, c0:c0 + n]),
                      r=["stat1"], w=["stat1"])

        push_wres(w_out_ab)
        dsl = {}

        def do_conv(j):
            if j + 1 < 8:
                dsl[j + 1] = build_diag(j + 1)
            conv_a(j, dsl[j])
            if q == NQ - 1:
                if j < 4:
                    samp_mm(j)
                    if j + 1 < 4:
                        samp_load(j + 1)
                elif j == 4:
                    samp_fin_a()
                    samp_load_b()
                elif j == 5:
                    samp_mm_b()
            if j == 7:
                cp("pool", halo_a, a2[:, :, 512:542], r=A2ALL, w=["halo_a"])

        sec_gate_val(0)
        dsl[0] = build_diag(0)
        sec_gate_val(1)
        if q == 0:
            prologue_deferred()
        if q == NQ - 1:
            samp_load(0)
        do_conv(0); sec_cpre_h(0); do_conv(1)
        sec_gate_val(2)
        do_conv(2); sec_cpre_h(1); do_conv(3)
        sec_gate_val(3)
        if q == 0:
            dbg("a2", a2, A2ALL, BF16)
        do_conv(4); sec_cpre_h(2); do_conv(5)
        sec_b_rest(0)
        sec_cpre_h(3)
        cp("pool", halo_c, ch[:, :, 512:514], r=CHALL, w=["halo_c"])
        flush_pending()
        do_conv(6)
        sec_b_rest(1)
        sec_b_rest(2)
        do_conv(7)
        ln_a_stats()
        sec_b_rest(3)
        for pr in range(4):
            sec_ga(pr)
        if q == NQ - 1:
            tails_out()
        if q == 0:
            dbg("ch", ch, CHALL, BF16)
            dbg("y0", yb, YALL, BF16)
            dbg("mean", stat[0], ["stat0"])
            dbg("rstd", stat[1], ["stat1"])

        def l0_out(i):
            np_ = 128 if i < 4 else NS
            c0 = i * 128
            pb = 2 * (i % 2)
            po = psum[0:np_, pb * 512:(pb + 2) * 512]
            pk = ["pb%d" % pb, "pb%d" % (pb + 1)]
            e_early = list(range(8, 16)) + list(range(0, 6))
            e_late = [6, 7]
            for hh in range(2):
                mm_group(po[:, hh * 512:(hh + 1) * 512],
                         [(yb[:, e, c0:c0 + np_], wres[:, e, hh * 512:(hh + 1) * 512]) for e in e_early],
                         r=[yk(e) for e in e_early] + ["wres"], w=[pk[hh]], last=False)
            for hh in range(2):
                mm_group(po[:, hh * 512:(hh + 1) * 512],
                         [(yb[:, e, c0:c0 + np_], wres[:, e, hh * 512:(hh + 1) * 512]) for e in e_late],
                         r=[yk(e) for e in e_late] + ["wres"], w=[pk[hh]], first=False)
            s = i % 2
            src = x_p[t0 + c0:t0 + c0 + 128, :] if i < 4 else x_s
            dma("sp", "d_xt", xtok[s][0:np_, :], src, w=["xtok%d" % s])
            stg, stgk = xin[i % 4]
            resid_ln(0, i, np_, xtok[s][0:np_, :], ["xtok%d" % s], po, pk, x1[0:np_, i, :], "x1_%d" % i,
                     bfout=stg[0:np_, :], bfkey=stgk)

        def l0_tr(i):
            np_ = 128 if i < 4 else NS
            c0 = i * 128
            s = i % 2
            stg, stgk = xin[i % 4]
            pst = bank_bf(4)[:, 0:8 * np_].rearrange("p (k t) -> p k t", k=8)
            tr_group([(pst[:, k, :], stg[0:np_, k * 128:(k + 1) * 128], idb[0:np_, 0:np_]) for k in range(8)],
                     r=[stgk, "idb"], w=["pb4"])
            cp("act", x1T[:, :, c0:c0 + np_], pst, r=["pb4"], w=["x1T"])

        for c in range(3, 8):
            load_wv_chunk(c)
        pre1 = [load_wgroup(w_in_c, 4096 + g * GW) for g in range(NWS)]
        for i in range(4):
            l0_out(i)
        for i in range(4):
            l0_tr(i)
        if ntt == 5:
            l0_out(4)
            l0_tr(4)
        if q == 0:
            dbg("x1", x1, ["x1_%d" % i for i in range(4)])
            dbg("x1T", x1T, ["x1T"], BF16)

        dma("sp", "d_in", nvg, norm_v_g.partition_broadcast(128), w=["nvg"])
        pi_i[0] = 0
        tb_i = [0]
        NG1 = 2048 // GW
        hoist = [None]
        for c in range(3):
            load_wv_chunk(c)
        while aff_pending:
            aff_pending.pop(0)()
        def l1_V(i):
            vfx, vk = vfs[i % 2]
            np_ = 128 if i < 4 else NS
            c0 = i * 128
            for cb in range(4):
                vb = cb % 2
                mm_group(psum[0:np_, vb * 512:(vb + 1) * 512],
                         [(x1T[:, k, c0:c0 + np_], wvc[:, 2 * cb:2 * cb + 2, k, :]) for k in range(KT)],
                         r=["x1T"] + WVALL, w=["pb%d" % vb])
                act(vfx[0:np_, cb * 512:(cb + 1) * 512], psum[0:np_, vb * 512:(vb + 1) * 512], AF.Gelu, r=["pb%d" % vb], w=[vk])

        def l1_C(i):
            l1_C1(i)
            l1_C2(i)

        def l1_C1(i):
            vfx, vk = vfs[i % 2]
            np_ = 128 if i < 4 else NS
            vi = i % 2
            mi = i % 2
            for hh in range(4):
                P.add("dve", lambda e, hh=hh: e.bn_stats(out=bnst[mi][0:np_, hh * 6:(hh + 1) * 6], in_=vfx[0:np_, hh * 512:(hh + 1) * 512]),
                      r=[vk], w=["bnst%d" % mi])
            P.add("dve", lambda e: e.bn_aggr(out=mv[mi][0:np_, 4:6], in_=bnst[mi][0:np_, 0:24]), r=["bnst%d" % mi], w=["mv%d" % mi])
            ts("dve", mv[mi][0:np_, 6:7], mv[mi][0:np_, 5:6], EPS, None, ALU.add, None, r=["mv%d" % mi], w=["mv%d" % mi])
            dve_rsqrt(mv[mi], np_, 6, 7, 11, "mv%d" % mi)
            if i == 0:
                ts("dve", vfx[0:np_, :], vfx[0:np_, :], mv[mi][0:np_, 4:5], mv[mi][0:np_, 7:8], ALU.subtract, ALU.mult,
                   r=[vk, "mv%d" % mi], w=[vk])
            else:
                ts("dve", mv[mi][0:np_, 9:10], mv[mi][0:np_, 4:5], mv[mi][0:np_, 7:8], -1.0, ALU.mult, ALU.mult, r=["mv%d" % mi], w=["mv%d" % mi])
                act(vfx[0:np_, :], vfx[0:np_, :], AF.Identity, r=[vk, "mv%d" % mi], w=[vk], bias=mv[mi][0:np_, 9:10], scale=mv[mi][0:np_, 7:8])

        def l1_C2(i):
            vfx, vk = vfs[i % 2]
            np_ = 128 if i < 4 else NS
            vi = i % 2
            if i < 4:
                tt("dve", vn[vi][0:np_, :], vfx[0:np_, :], nvg[0:np_, :], ALU.mult, r=[vk, "nvg"], w=["vn%d" % vi])
            else:
                dma("sp", "d_in", nvb[0:np_, :], norm_v_b.partition_broadcast(np_), w=["nvb"])
                tt("dve", vfx[0:np_, :], vfx[0:np_, :], nvg[0:np_, :], ALU.mult, r=[vk, "nvg"], w=[vk])
                cp("dve", vn[vi][0:np_, :], vfx[0:np_, :], r=[vk], w=["vn%d" % vi])
                tt("dve", vfx[0:np_, :], vfx[0:np_, :], nvb[0:np_, :], ALU.add, r=[vk, "nvb"], w=[vk])
                dma("sp", "d_out", vcs, vfx[0:np_, :], r=[vk])

        def l1_S(i):
            np_ = 128 if i < 4 else NS
            c0 = i * 128
            vi = i % 2
            for b4 in range(4):
                pbm = 2 + b4

                def fn(e, b4=b4, pbm=pbm):
                    ins = None
                    for dd in range(4):
                        dt_ = b4 * 4 + dd
                        h = dt_ // 2
                        out = psum[:, pbm * 512 + dd * 128:pbm * 512 + dd * 128 + np_]
                        if i < 4:
                            rh, b4r = R[:, h, :], R4[:, h, :]
                        else:
                            rh, b4r = Rs[0:NS, h, :], R4s[:, h, :]
                        e.matmul(out, lhsT=vn[vi][0:np_, dt_ * 128:(dt_ + 1) * 128], rhs=rh, start=True, stop=False)
                        ins = e.matmul(out, lhsT=L4[:, dt_ * 128:(dt_ + 1) * 128], rhs=b4r, start=False, stop=True)
                    return ins
                P.add("pe", fn, r=["vn%d" % vi, "R", "Rs", "R4", "R4s", "L4"], w=["pb%d" % pbm])
                pm = psum[:, pbm * 512:(pbm + 1) * 512].rearrange("p (a t) -> p a t", a=4)[:, :, 0:np_]
                yv = yb[:, b4 * 4:b4 * 4 + 4, c0:c0 + np_]
                tt("dve", yv, yv, pm, ALU.mult, r=[yk(b4 * 4 + d_) for d_ in range(4)] + ["pb%d" % pbm],
                   w=[yk(b4 * 4 + d_) for d_ in range(4)])

        def l1_O(i):
            np_ = 128 if i < 4 else NS
            c0 = i * 128
            po = psum[0:np_, 6 * 512:8 * 512]
            pk = ["pb6", "pb7"]
            for hh in range(2):
                mm_group(po[:, hh * 512:(hh + 1) * 512],
                         [(yb[:, e, c0:c0 + np_], wres[:, e, hh * 512:(hh + 1) * 512]) for e in range(16)],
                         r=YALL + ["wres"], w=[pk[hh]])
            s = i % 2
            resid_ln(1, i, np_, x1[0:np_, i, :], ["x1_%d" % i], po, pk, o32[s][0:np_, :], "o32%d" % s)
            dst = y_p[t0 + c0:t0 + c0 + 128, :] if i < 4 else y_s
            dma("sp", "d_out", dst, o32[s][0:np_, :], r=["o32%d" % s])


        def _hoisted():
            l1_V(0)
            l1_C(0)
        hoist[0] = _hoisted
        for g4 in range(NG1):
            sgg = pre1[g4] if g4 < len(pre1) else load_wgroup(w_in_c, 4096 + g4 * GW)
            for e in range(NE):
                et = g4 * NE + e

                def cons_gg(si, c0, n, pap, pkey, et=et):
                    act(yb[:, et, c0:c0 + n], pap, AF.Silu, r=[pkey], w=[yk(et)])
                inproj(x1T, "x1T", sgg, e, q, cons_gg, nbank=4)
            if g4 == 3:
                hoist[0]()
        for g4 in range(NG1):
            su = load_wgroup(w_in_c, g4 * GW)
            for e in range(NE):
                et = g4 * NE + e
                tb = tb_i[0] % 2
                tb_i[0] += 1

                def cons_u(si, c0, n, pap, pkey, et=et, tb=tb):
                    act(tmpB[tb][:, c0:c0 + n], pap, AF.Gelu, r=[pkey], w=["tmpB%d" % tb])
                    tt("dve", yb[:, et, c0:c0 + n], tmpB[tb][:, c0:c0 + n], yb[:, et, c0:c0 + n], ALU.mult,
                       r=["tmpB%d" % tb, yk(et)], w=[yk(et)])
                inproj(x1T, "x1T", su, e, q, cons_u, nbank=4)
        push_wres(w_out_c)
        flush_pending()
        if q == 0:
            dbg("ugs", yb, YALL, BF16)

        if ntt > 1:
            l1_V(1)
        for i in range(ntt):
            if i + 2 < ntt:
                l1_V(i + 2)
            if i + 1 < ntt:
                l1_C1(i + 1)
            l1_S(i)
            if i + 1 < ntt:
                l1_C2(i + 1)
            if i + 2 == ntt - 1 and q + 1 < NQ:
                prefetch_x_dma(q + 1)
                prefetch_l0_weights()
            if i >= 2:
                l1_O(i - 2)
        for i in range(max(ntt - 2, 0), ntt):
            if i == ntt - 1 and q + 1 < NQ:
                prefetch_x_tr(q + 1)
            l1_O(i)

    prefetch_x_dma(0)
    prefetch_l0_weights()
    dma("sp", "d_in", idf, identd, w=["idf"])
    dma("sp", "d_in", trl, trild, w=["x1_2"])
    dma("sp", "d_in", p37[0:31, :], conv_a_w, w=["x1_0"])
    dma("sp", "d_in", p37[31:32, :], conv_a_b, w=["x1_0"])
    dma("sp", "d_in", p37[32:33, :], norm_a_g, w=["x1_0"])
    dma("sp", "d_in", p37[33:34, :], norm_a_b, w=["x1_0"])
    dma("sp", "d_in", p37[34:37, :], conv_b_w, w=["x1_0"])
    for l in range(2):
        dma("sp", "d_in", lng[l], ln_g[l].partition_broadcast(128), w=["lng%d" % l])
        dma("sp", "d_in", lnb[l], ln_b[l].partition_broadcast(128), w=["lnb%d" % l])
    dma("sp", "d_in", wscf, w_s_c.rearrange("h t s -> t h s"), w=["x1_1"])
    dma("sp", "d_in", w00[0:16, :], w_s_c[:, 0, 0].partition_broadcast(16), w=["w00"], allow_slow_non_contiguous=True)
    dma("sp", "d_in", bsf[0:1, 0:1024], b_s_c, w=["x1_3"])

    cp("dve", idb, idf, r=["idf"], w=["idb"])
    memset("dve", ones_b, 1.0, w=["ones"])
    memset("dve", m05, -0.5, w=["m05"])
    memset("dve", a2[:, :, 0:30], 0.0, w=A2ALL)
    memset("dve", ch[:, :, 0:2], 0.0, w=CHALL)
    for j in range(8):
        tr_group([(psum[:, 3584 + j * 37: 3584 + (j + 1) * 37], p37[0:37, j * 128:(j + 1) * 128], idf[0:37, 0:37])],
                 r=["x1_0", "idf"], w=["pb7"])
    cp("dve", prm, psum[:, 3584:3584 + 8 * 37].rearrange("p (j r) -> p j r", j=8), r=["pb7"], w=["prm"])
    ts("dve", prmh, prm[:, :, 0:31], 0.5, None, ALU.mult, None, r=["prm"], w=["prmh"])
    prefetch_x_tr(0)
    for q in range(NQ):
        quarter(q)

    semnames = sorted(P.cnt.keys())
    sems = {n: es.enter_context(nc.semaphore(n)) for n in semnames}
    block = es.enter_context(nc.Block())
    engmap = {"pe": "tensor", "act": "scalar", "dve": "vector", "pool": "gpsimd", "sp": "sync"}

    def make_body(en):
        def body(e):
            waited = {}
            for deps, fn, sem, inc in P.ops[en]:
                for (sn, val) in deps:
                    if en == "pe" and sn == "pe":
                        continue
                    if waited.get(sn, 0) >= val:
                        continue
                    e.wait_ge(sems[sn], val)
                    waited[sn] = val
                ins = fn(e)
                ins.then_inc(sems[sem], inc)
            if en == "sp":
                for sn in semnames:
                    e.wait_ge(sems[sn], P.cnt[sn])
        return body

    for en in ENGS:
        getattr(block, engmap[en])(make_body(en))
    es.close()
    nc.dbg_list = dbg_list
    return nc


_NC_CACHE = {}


def kernel(x_prompt, x_sample, cache_conv_a, cache_conv_b, w_in_ab, conv_a_w, conv_a_b, norm_a_g, norm_a_b,
           conv_b_w, w_out_ab, w_in_c, w_s_c, b_s_c, norm_v_g, norm_v_b, w_out_c, ln_g, ln_b):
    f = lambda a: np.ascontiguousarray(np.asarray(a, dtype=np.float32))
    x_prompt = f(x_prompt); x_sample = f(x_sample)
    cache_conv_a = f(cache_conv_a); cache_conv_b = f(cache_conv_b)
    shared = {
        "w_in_ab": f(w_in_ab)[0], "conv_a_w": f(conv_a_w)[0], "conv_a_b": f(conv_a_b).reshape(1, D),
        "norm_a_g": f(norm_a_g).reshape(1, D), "norm_a_b": f(norm_a_b).reshape(1, D), "conv_b_w": f(conv_b_w)[0],
        "w_out_ab": f(w_out_ab)[0], "w_in_c": f(w_in_c)[0], "w_s_c": f(w_s_c)[0], "b_s_c": f(b_s_c).reshape(1, 1024),
        "norm_v_g": f(norm_v_g).reshape(2048), "norm_v_b": f(norm_v_b).reshape(2048), "w_out_c": f(w_out_c)[0],
        "ln_g": f(ln_g), "ln_b": f(ln_b),
        "identd": np.eye(128, dtype=np.float32), "trild": np.tril(np.ones((128, 128), dtype=np.float32)),
    }
    in_maps = []
    for c in range(NCORES):
        m = dict(shared)
        m["x_p"] = x_prompt[c]
        m["x_s"] = np.ascontiguousarray(x_sample[NS * c:NS * (c + 1), 0, :])
        m["cca"] = np.ascontiguousarray(cache_conv_a[0, NS * c:NS * (c + 1)])
        m["ccb"] = np.ascontiguousarray(cache_conv_b[0, NS * c:NS * (c + 1)])
        in_maps.append(m)
    if "nc" not in _NC_CACHE:
        _NC_CACHE["nc"] = build_program()
    nc = _NC_CACHE["nc"]
    res = run_bass_kernel_spmd(nc, in_maps, core_ids=list(range(NCORES)))
    rs = res.results
    y_prompt = np.stack([rs[c]["y_p"] for c in range(NCORES)], 0)
    y_sample = np.concatenate([rs[c]["y_s"] for c in range(NCORES)], 0)[:, None, :]
    ca_p = np.stack([rs[c]["cap"] for c in range(NCORES)], 0)[None]
    ca_s = np.concatenate([rs[c]["cas"] for c in range(NCORES)], 0)[None]
    cb_p = np.stack([rs[c]["cbp"] for c in range(NCORES)], 0)[None]
    cb_s = np.concatenate([rs[c]["cbs"] for c in range(NCORES)], 0)[None]
    v_s = np.concatenate([rs[c]["vcs"] for c in range(NCORES)], 0)[None, :, None, :]
    return (y_prompt.astype(np.float32), y_sample.astype(np.float32), ca_p.astype(np.float32), ca_s.astype(np.float32),
            cb_p.astype(np.float32), cb_s.astype(np.float32), v_s.astype(np.float32))
```
